# Optimizing a Trainium2 kernel written in Bass

```python
import jax, jax.numpy as jnp
from jax import lax
import numpy as np

D_MODEL = 1024
BATCH = 32
SEQ = 256
DEPTH = 2
DEC_BATCH = 2
DEC_SEQ = 4096
PAST_LEN = 256

GRID_W = 64
CONV_W = 3
M_WIDTH = D_MODEL // 2
M_HEADS = 4
M_HEAD_DIM = M_WIDTH // M_HEADS
M_CHUNK = 64
R_WIDTH = D_MODEL // 2
R_HEAD_DIM = 64
R_HEADS = R_WIDTH // R_HEAD_DIM
W_LORA = 64
A_LORA = 64
G_LORA = 128
R_IN = 3 * R_WIDTH + 2 * W_LORA + 2 * A_LORA + G_LORA
FFN_HIDDEN = -(-(8 * D_MODEL) // (3 * 256)) * 256
IN_SIZES = (2 * M_WIDTH, M_WIDTH, M_WIDTH, 4 * M_HEADS, R_IN, 2 * D_MODEL)
P_IN = 2 * M_WIDTH + M_WIDTH + M_WIDTH + 4 * M_HEADS + R_IN + 2 * D_MODEL
RMS_EPS = 1e-6
GN_EPS = 64e-5

kernel_name = 'hybrid_mlstm_rwkv7_diffusion_step'


def _heads(x, n_heads):
    return x.reshape(x.shape[0], x.shape[1], n_heads, -1)


def _flat(x):
    return x.reshape(x.shape[0], x.shape[1], -1)


def _flip(x):
    return jnp.flip(x, axis=1)


def _split(x, sizes):
    return jnp.split(x, [int(s) for s in np.cumsum(sizes)[:-1]], axis=-1)


def rmsnorm(x, g):
    xf = x.astype(jnp.float32)
    y = xf * lax.rsqrt(jnp.mean(xf * xf, axis=-1, keepdims=True) + RMS_EPS)
    return (y * g.astype(jnp.float32)).astype(x.dtype)


def dwconv(x, w, grid):
    B, T, C = x.shape
    w = w.astype(x.dtype)
    if grid:
        rows = T // GRID_W
        y = lax.conv_general_dilated(x.reshape(B, rows, GRID_W, C), w[:, :, None, :], (1, 1), 'SAME',
                                     dimension_numbers=('NHWC', 'HWIO', 'NHWC'), feature_group_count=C)
    else:
        y = lax.conv_general_dilated(x, w[CONV_W // 2][:, None, :], (1,), 'SAME',
                                     dimension_numbers=('NWC', 'WIO', 'NWC'), feature_group_count=C)
    return y.reshape(B, T, C)


def mlstm_chunkwise(q, k, v, ig, lf, C0, n0, m0):
    f32 = jnp.float32
    B, T, H, Dh = q.shape
    nc = T // M_CHUNK

    def blocks(a):
        a = a.astype(f32).reshape((B, nc, M_CHUNK, H) + a.shape[3:])
        return jnp.swapaxes(jnp.swapaxes(a, 0, 1), 2, 3)

    mask = jnp.tril(jnp.ones((M_CHUNK, M_CHUNK), dtype=bool))

    def step(carry, inp):
        C, n, m = carry
        qc, kc, vc, ic, fc = inp
        b = jnp.cumsum(fc, axis=-1)
        logw = jnp.where(mask, b[..., :, None] - b[..., None, :] + ic[..., None, :], -jnp.inf)
        inter = b + m[..., None]
        mt = jnp.maximum(inter, jnp.max(logw, axis=-1))
        w = jnp.exp(logw - mt[..., None])
        wi = jnp.exp(inter - mt)
        s = jnp.einsum('bhtd,bhsd->bhts', qc, kc) * w
        num = jnp.einsum('bhts,bhsd->bhtd', s, vc) + wi[..., None] * jnp.einsum('bhtd,bhde->bhte', qc, C)
        den = jnp.sum(s, axis=-1) + wi * jnp.einsum('bhtd,bhd->bht', qc, n)
        h = num / jnp.maximum(jnp.abs(den), jnp.exp(-mt))[..., None]
        bl = b[..., -1:]
        le = bl - b + ic
        ie = bl[..., 0] + m
        mn = jnp.maximum(ie, jnp.max(le, axis=-1))
        we = jnp.exp(le - mn[..., None])
        wie = jnp.exp(ie - mn)
        C = wie[..., None, None] * C + jnp.einsum('bhs,bhsd,bhse->bhde', we, kc, vc)
        n = wie[..., None] * n + jnp.einsum('bhs,bhsd->bhd', we, kc)
        return (C, n, mn), h

    xs = tuple(blocks(a) for a in (q, k, v, ig, lf))
    (C, n, m), h = lax.scan(step, (C0.astype(f32), n0.astype(f32), m0.astype(f32)), xs)
    h = jnp.swapaxes(jnp.swapaxes(h, 2, 3), 0, 1).reshape(B, T, H, Dh)
    return h, C, n, m


def rwkv7_scan(r, decay, k, v, a_vec, b_vec, S0):
    f32 = jnp.float32
    xs = tuple(jnp.moveaxis(t.astype(f32), 1, 0) for t in (r, decay, k, v, a_vec, b_vec))

    def step(S, inp):
        r_t, w_t, k_t, v_t, a_t, b_t = inp
        sa = jnp.einsum('bhvk,bhk->bhv', S, a_t)
        S = S * w_t[:, :, None, :] + sa[..., None] * b_t[:, :, None, :] + v_t[..., None] * k_t[:, :, None, :]
        return S, jnp.einsum('bhvk,bhk->bhv', S, r_t)

    S, y = lax.scan(step, S0.astype(f32), xs)
    return jnp.moveaxis(y, 0, 1), S


def token_mixer(h, p, grid, st):
    f32 = jnp.float32
    B, T, _ = h.shape
    u = h @ p['w_in']
    u_qk, u_v, u_o, u_gate, u_r, u_merge = _split(u, IN_SIZES)

    qk = jax.nn.silu(dwconv(u_qk, p['m_conv'], grid))
    q, k = jnp.split(qk, 2, axis=-1)
    q = _heads(q, M_HEADS) * (M_HEAD_DIM ** -0.5)
    k = _heads(k, M_HEADS)
    v = _heads(u_v, M_HEADS)
    g = u_gate.astype(f32).reshape(B, T, 4, M_HEADS) + p['m_gate_b'].astype(f32)
    hf, Cf, nf, mf = mlstm_chunkwise(q, k, v, g[:, :, 0], jax.nn.log_sigmoid(g[:, :, 1]), st[0], st[1], st[2])
    hb, Cb, nb, mb = mlstm_chunkwise(_flip(q), _flip(k), _flip(v), _flip(g[:, :, 2]),
                                     _flip(jax.nn.log_sigmoid(g[:, :, 3])), st[3], st[4], st[5])
    hm = hf + _flip(hb)
    hm = hm * lax.rsqrt(jnp.mean(hm * hm, axis=-1, keepdims=True) + RMS_EPS)
    hm = (_flat(hm) * p['m_norm_g'].astype(f32) * jax.nn.sigmoid(u_o.astype(f32))).astype(h.dtype)

    rx = dwconv(u_r, p['r_conv'], grid)
    r, kr, vr, wl_f, wl_b, al_f, al_b, gl = _split(rx, (R_WIDTH, R_WIDTH, R_WIDTH, W_LORA, W_LORA, A_LORA, A_LORA, G_LORA))
    kk = _heads((kr * p['r_kk']).astype(f32), R_HEADS)
    kk = kk / jnp.maximum(jnp.linalg.norm(kk, axis=-1, keepdims=True), 1e-12)
    kk_flat = _flat(kk)
    ys, Ss = [], []
    for d, (wl, al) in enumerate(((wl_f, al_f), (wl_b, al_b))):
        w_raw = (p['r_w0'][d] + jnp.tanh(wl) @ p['r_w2'][d]).astype(f32)
        decay = jnp.exp(-jnp.exp(-jax.nn.softplus(-w_raw) - 0.5))
        a = jax.nn.sigmoid((p['r_a0'][d] + al @ p['r_a2'][d]).astype(f32))
        ktil = kr.astype(f32) * (1.0 + (a - 1.0) * p['r_ka'].astype(f32))
        args = [_heads(t, R_HEADS) for t in (r, decay, ktil, vr, -kk_flat, kk_flat * a)]
        if d == 1:
            args = [_flip(t) for t in args]
        y, S = rwkv7_scan(*args, st[6 + d])
        ys.append(_flip(y) if d == 1 else y)
        Ss.append(S)
    y = ys[0] + ys[1]
    mu = jnp.mean(y, axis=-1, keepdims=True)
    var = jnp.mean(jnp.square(y - mu), axis=-1, keepdims=True)
    yn = _flat((y - mu) * lax.rsqrt(var + GN_EPS)) * p['r_gn_w'].astype(f32) + p['r_gn_b'].astype(f32)
    rk = p['r_rk'].astype(f32).reshape(R_HEADS, R_HEAD_DIM)
    r_h, k_h, v_h = (_heads(t.astype(f32), R_HEADS) for t in (r, kr, vr))
    bonus = _flat(jnp.sum(r_h * k_h * rk, axis=-1, keepdims=True) * v_h)
    gate = jax.nn.sigmoid(gl) @ p['r_g2']
    hr = ((yn + bonus) * gate).astype(h.dtype)

    gm, gr = jnp.split(u_merge, 2, axis=-1)
    merged = jax.nn.sigmoid(gm) * (hm @ p['proj_m']) + jax.nn.sigmoid(gr) * (hr @ p['proj_r'])
    return merged @ p['w_out'], (Cf, nf, mf, Cb, nb, mb, Ss[0], Ss[1])


def swiglu(h, w1, w3, w2):
    return (jax.nn.silu(h @ w1) * (h @ w3)) @ w2


def trunk_layer(x, mod, p, grid, st):
    sh1, sc1, g1, sh2, sc2, g2 = jnp.split(mod, 6, axis=-1)
    h = rmsnorm(x, p['norm1_g']) * (1 + sc1) + sh1
    mix, st_new = token_mixer(h, p, grid, st)
    x = x + g1 * mix
    h = rmsnorm(x, p['norm2_g']) * (1 + sc2) + sh2
    x = x + g2 * swiglu(h, p['ffn_w1'], p['ffn_w3'], p['ffn_w2'])
    return x, st_new


def setup_inputs(seed: int = 0) -> dict:
    key = jax.random.key(seed)
    ks = iter(jax.random.split(key, 48))
    f32 = jnp.float32

    def nrm(shape, scale):
        return jax.random.normal(next(ks), shape, f32) * scale

    def conv_init(shape):
        return nrm(shape, 0.2).at[:, CONV_W // 2, CONV_W // 2, :].add(1.0)

    def fbias():
        return 3.0 + 3.0 * jax.random.uniform(next(ks), (DEPTH, M_HEADS), f32)

    m_gate_b = jnp.stack([nrm((DEPTH, M_HEADS), 0.1), fbias(), nrm((DEPTH, M_HEADS), 0.1), fbias()], axis=1)
    return {
        'x_prompt': nrm((BATCH, SEQ, D_MODEL), 1.0),
        'x_sample': nrm((DEC_BATCH, DEC_SEQ, D_MODEL), 1.0),
        'state_mlstm_c': nrm((DEC_BATCH, DEPTH, 2, M_HEADS, M_HEAD_DIM, M_HEAD_DIM), 0.1),
        'state_mlstm_n': nrm((DEC_BATCH, DEPTH, 2, M_HEADS, M_HEAD_DIM), 0.1),
        'state_mlstm_m': nrm((DEC_BATCH, DEPTH, 2, M_HEADS), 1.0),
        'state_rwkv': nrm((DEC_BATCH, DEPTH, 2, R_HEADS, R_HEAD_DIM, R_HEAD_DIM), 0.1),
        'c': nrm((DEC_BATCH, D_MODEL), 1.0),
        'c_ctx': nrm((D_MODEL,), 1.0),
        'ada_w': nrm((DEPTH, D_MODEL, 6 * D_MODEL), 0.5 * D_MODEL ** -0.5),
        'ada_b': nrm((DEPTH, 6 * D_MODEL), 0.05),
        'norm1_g': 1.0 + nrm((DEPTH, D_MODEL), 0.05),
        'norm2_g': 1.0 + nrm((DEPTH, D_MODEL), 0.05),
        'w_in': nrm((DEPTH, D_MODEL, P_IN), D_MODEL ** -0.5),
        'm_conv': conv_init((DEPTH, CONV_W, CONV_W, 2 * M_WIDTH)),
        'm_gate_b': m_gate_b,
        'm_norm_g': 1.0 + nrm((DEPTH, M_WIDTH), 0.05),
        'r_conv': conv_init((DEPTH, CONV_W, CONV_W, R_IN)),
        'r_w0': nrm((DEPTH, 2, R_WIDTH), 0.5),
        'r_w2': nrm((DEPTH, 2, W_LORA, R_WIDTH), 0.1),
        'r_a0': nrm((DEPTH, 2, R_WIDTH), 0.1),
        'r_a2': nrm((DEPTH, 2, A_LORA, R_WIDTH), A_LORA ** -0.5),
        'r_g2': nrm((DEPTH, G_LORA, R_WIDTH), G_LORA ** -0.5),
        'r_kk': 0.85 + nrm((DEPTH, R_WIDTH), 0.05),
        'r_ka': 1.0 + nrm((DEPTH, R_WIDTH), 0.05),
        'r_rk': nrm((DEPTH, R_WIDTH), 0.1),
        'r_gn_w': 1.0 + nrm((DEPTH, R_WIDTH), 0.05),
        'r_gn_b': nrm((DEPTH, R_WIDTH), 0.02),
        'proj_m': nrm((DEPTH, M_WIDTH, D_MODEL), M_WIDTH ** -0.5),
        'proj_r': nrm((DEPTH, R_WIDTH, D_MODEL), R_WIDTH ** -0.5),
        'w_out': nrm((DEPTH, D_MODEL, D_MODEL), D_MODEL ** -0.5),
        'ffn_w1': nrm((DEPTH, D_MODEL, FFN_HIDDEN), D_MODEL ** -0.5),
        'ffn_w3': nrm((DEPTH, D_MODEL, FFN_HIDDEN), D_MODEL ** -0.5),
        'ffn_w2': nrm((DEPTH, FFN_HIDDEN, D_MODEL), FFN_HIDDEN ** -0.5),
        'final_g': 1.0 + nrm((D_MODEL,), 0.05),
    }


def reference(x_prompt, x_sample, state_mlstm_c, state_mlstm_n, state_mlstm_m, state_rwkv, c, c_ctx,
              ada_w, ada_b, norm1_g, norm2_g, w_in, m_conv, m_gate_b, m_norm_g, r_conv, r_w0, r_w2,
              r_a0, r_a2, r_g2, r_kk, r_ka, r_rk, r_gn_w, r_gn_b, proj_m, proj_r, w_out,
              ffn_w1, ffn_w3, ffn_w2, final_g):
    f32 = jnp.float32

    def layer_params(l):
        return dict(ada_w=ada_w[l], ada_b=ada_b[l], norm1_g=norm1_g[l], norm2_g=norm2_g[l], w_in=w_in[l],
                    m_conv=m_conv[l], m_gate_b=m_gate_b[l], m_norm_g=m_norm_g[l], r_conv=r_conv[l],
                    r_w0=r_w0[l], r_w2=r_w2[l], r_a0=r_a0[l], r_a2=r_a2[l], r_g2=r_g2[l], r_kk=r_kk[l],
                    r_ka=r_ka[l], r_rk=r_rk[l], r_gn_w=r_gn_w[l], r_gn_b=r_gn_b[l], proj_m=proj_m[l],
                    proj_r=proj_r[l], w_out=w_out[l], ffn_w1=ffn_w1[l], ffn_w3=ffn_w3[l], ffn_w2=ffn_w2[l])

    bp = x_prompt.shape[0]
    m_zero = (jnp.zeros((bp, M_HEADS, M_HEAD_DIM, M_HEAD_DIM), f32), jnp.zeros((bp, M_HEADS, M_HEAD_DIM), f32),
              jnp.zeros((bp, M_HEADS), f32))
    r_zero = jnp.zeros((bp, R_HEADS, R_HEAD_DIM, R_HEAD_DIM), f32)
    zero_st = m_zero + m_zero + (r_zero, r_zero)
    x = x_prompt
    ctx_states = []
    for l in range(DEPTH):
        p = layer_params(l)
        mod = (jax.nn.silu(c_ctx) @ p['ada_w'] + p['ada_b'])[None, None, :]
        x, st = trunk_layer(x, mod, p, False, zero_st)
        ctx_states.append(st)
    y_prompt = rmsnorm(x, final_g)
    new_mlstm_c = jnp.stack([jnp.stack([s[0], s[3]], axis=1) for s in ctx_states], axis=1)
    new_mlstm_n = jnp.stack([jnp.stack([s[1], s[4]], axis=1) for s in ctx_states], axis=1)
    new_mlstm_m = jnp.stack([jnp.stack([s[2], s[5]], axis=1) for s in ctx_states], axis=1)
    new_rwkv_s = jnp.stack([jnp.stack([s[6], s[7]], axis=1) for s in ctx_states], axis=1)

    xs = x_sample
    for l in range(DEPTH):
        p = layer_params(l)
        mod = (jax.nn.silu(c) @ p['ada_w'] + p['ada_b'])[:, None, :]
        st = (state_mlstm_c[:, l, 0], state_mlstm_n[:, l, 0], state_mlstm_m[:, l, 0],
              state_mlstm_c[:, l, 1], state_mlstm_n[:, l, 1], state_mlstm_m[:, l, 1],
              state_rwkv[:, l, 0], state_rwkv[:, l, 1])
        xs, _ = trunk_layer(xs, mod, p, True, st)
    y_sample = rmsnorm(xs, final_g)
    return (y_prompt, y_sample, new_mlstm_c, new_mlstm_n, new_mlstm_m, new_rwkv_s)
```

```python
from contextlib import ExitStack
import os
import re
FCUT = int(os.environ.get('FCUT', '99'))
import numpy as np
import concourse.bass as bass
import concourse.mybir as mybir
from concourse.bass_utils import run_bass_kernel_spmd

F32 = mybir.dt.float32
ALU = mybir.AluOpType
AF = mybir.ActivationFunctionType
AX = mybir.AxisListType


class _Op:
    __slots__ = ("eng", "fn", "deps", "is_dma", "needs_inc", "sem", "val", "extra_waits")


class Sched:
    ENGS = ("pe", "act", "dve", "pool", "sp")

    def __init__(self, nc, n_dma_sems=24):
        self.nc = nc
        self.ops = []
        self.lastw = {}
        self.readers = {}
        self.last_on = {}
        self.stack = ExitStack()
        self.n_dma_sems = n_dma_sems
        self._nid = 0

    def sb(self, name, shape, dtype=F32):
        return self.stack.enter_context(self.nc.sbuf_tensor(name, list(shape), dtype))

    def ps(self, name, shape, dtype=F32):
        return self.stack.enter_context(self.nc.psum_tensor(name, list(shape), dtype))

    def add(self, eng, fn, reads=(), writes=(), dma=False):
        deps = set()
        for k in reads:
            d = self.lastw.get(k)
            if d is not None:
                deps.add(d)
        for k in writes:
            d = self.lastw.get(k)
            if d is not None:
                deps.add(d)
            deps.update(self.readers.get(k, ()))
        op = _Op()
        op.eng, op.fn, op.deps, op.is_dma = eng, fn, deps, dma
        op.needs_inc, op.sem, op.val, op.extra_waits = False, None, 0, []
        i = len(self.ops)
        self.ops.append(op)
        for k in reads:
            self.readers.setdefault(k, []).append(i)
        for k in writes:
            self.lastw[k] = i
            self.readers[k] = []
        return i

    def pe(self, fn, reads=(), writes=()):
        return self.add("pe", fn, reads, writes)

    def act(self, fn, reads=(), writes=()):
        return self.add("act", fn, reads, writes)

    def dve(self, fn, reads=(), writes=()):
        return self.add("dve", fn, reads, writes)

    def pool(self, fn, reads=(), writes=()):
        return self.add("pool", fn, reads, writes)

    def dma(self, out, in_, reads=(), writes=(), eng="sp", **kw):
        return self.add(eng, lambda e: e.dma_start(out=out, in_=in_, **kw), reads, writes, dma=True)

    def finish(self):
        nc, ops = self.nc, self.ops
        def synced(o, od):
            return od.is_dma or od.eng != o.eng or o.eng != "pe"
        for o in ops:
            for d in o.deps:
                od = ops[d]
                if synced(o, od):
                    od.needs_inc = True
        for o in ops:
            if o.is_dma:
                o.needs_inc = True
        esem = {e: self.stack.enter_context(nc.semaphore("s_" + e)) for e in self.ENGS}
        dsems = [self.stack.enter_context(nc.semaphore("d%d" % i)) for i in range(self.n_dma_sems)]
        cnt = {e: 0 for e in self.ENGS}
        dcnt = [0] * self.n_dma_sems
        dlast = [None] * self.n_dma_sems
        j = 0
        for o in ops:
            if not o.needs_inc:
                continue
            if o.is_dma:
                s = j % self.n_dma_sems
                j += 1
                if dlast[s] is not None:
                    o.extra_waits.append(dlast[s])
                dcnt[s] += 16
                o.sem, o.val = dsems[s], dcnt[s]
                dlast[s] = (dsems[s], dcnt[s])
            else:
                cnt[o.eng] += 1
                o.sem, o.val = esem[o.eng], cnt[o.eng]
        per = {e: [] for e in self.ENGS}
        for o in ops:
            per[o.eng].append(o)
        final_waits = [dl for dl in dlast if dl is not None]
        engobj = {"pe": "tensor", "act": "scalar", "dve": "vector", "pool": "gpsimd", "sp": "sync"}

        def emit(ename, eng):
            waited = {}
            for o in per[ename]:
                ws = list(o.extra_waits)
                for d in o.deps:
                    od = ops[d]
                    if synced(o, od):
                        ws.append((od.sem, od.val))
                best = {}
                for (s, v) in ws:
                    key = id(s)
                    if key not in best or best[key][1] < v:
                        best[key] = (s, v)
                for key, (s, v) in best.items():
                    if waited.get(key, 0) < v:
                        eng.wait_ge(s, v)
                        waited[key] = v
                ins = o.fn(eng)
                if o.needs_inc:
                    ins.then_inc(o.sem, 16 if o.is_dma else 1)
            if ename == "sp":
                for (s, v) in final_waits:
                    if waited.get(id(s), 0) < v:
                        eng.wait_ge(s, v)
                        waited[id(s)] = v

        with nc.Block() as block:
            for ename in self.ENGS:
                getattr(block, engobj[ename])(lambda eng, _n=ename: emit(_n, eng))
        self.stack.close()
        return nc


DEPTH, DM, PIN, FH = 2, 1024, 6032, 2816
NP_, NS_ = 1024, 4096
NT = 512
STORE_Q = "act"
NEG = -1.0e30
WSC = -0.6065306597126334


def _layout(entries):
    off, d = 0, {}
    for name, w in entries:
        d[name] = (off, w)
        off += w
    return d, off


def vec_layout():
    e = [("fg", 8)]
    for l in range(DEPTH):
        e += [(("ada_b", l), 48), (("n1g", l), 8), (("n2g", l), 8), (("mconv", l), 72), (("rconv", l), 135),
              (("a0", l, 0), 4), (("a0", l, 1), 4), (("kk", l), 4), (("ka", l), 4), (("rk", l), 4),
              (("gnw", l), 4), (("gnb", l), 4)]
    return _layout(e)


def row_layout():
    e = []
    for l in range(DEPTH):
        e += [(("gb", l), 16), (("mng", l), 512), (("w0", l, 0), 512), (("w0", l, 1), 512)]
    return _layout(e)


def const_layout():
    e = [("ident", 128), ("ones", 128), ("blk", 128)]
    for d in range(2):
        e += [(("tri", d), 64), (("mneg", d), 256), (("mpos", d), 256), (("tri2", d), 128),
              (("m1mask", d), 512), (("qmask", d), 512), (("akmask", d), 512)]
    return _layout(e)


def make_consts():
    lay, n = const_layout()
    c = np.zeros((128, n), np.float32)

    def put(name, arr):
        o, w = lay[name]
        a = np.asarray(arr, np.float32)
        c[: a.shape[0], o:o + w] = a.reshape(a.shape[0], w)
    put("ident", np.eye(128))
    put("ones", np.ones((128, 128)))
    blk = np.zeros((128, 128))
    blk[:64, :64] = 1
    blk[64:, 64:] = 1
    put("blk", blk)
    s = np.arange(64)[:, None]
    t = np.arange(64)[None, :]
    for d in range(2):
        inc = (s <= t) if d == 0 else (s >= t)
        strict = (s < t) if d == 0 else (s > t)
        put(("tri", d), inc.astype(np.float32))
        put(("mneg", d), np.tile(np.where(inc.T, 0.0, NEG)[:, None, :], (1, 4, 1)))
        put(("mpos", d), np.tile(np.where(inc, 0.0, -NEG)[:, None, :], (1, 4, 1)))
        put(("tri2", d), np.concatenate([inc, strict], 1).astype(np.float32) * WSC)
        m1 = np.concatenate([strict, inc], 1).astype(np.float32)
        m1 = np.concatenate([m1, m1], 0)
        put(("m1mask", d), np.tile(m1[:, None, :], (1, 4, 1)))
        put(("qmask", d), np.tile(strict.T.astype(np.float32)[:, None, :], (1, 8, 1)))
        put(("akmask", d), np.tile(strict.astype(np.float32)[:, None, :], (1, 8, 1)))
    return c


def fm(v, nchunk):
    return np.ascontiguousarray(np.asarray(v, np.float32).reshape(nchunk, 128).T)


def pack_vecs(inp):
    lay, n = vec_layout()
    c = np.zeros((128, n), np.float32)

    def put(name, arr):
        o, w = lay[name]
        c[:, o:o + w] = arr.reshape(128, w)
    put("fg", fm(inp["final_g"], 8))
    for l in range(DEPTH):
        put(("ada_b", l), fm(inp["ada_b"][l], 48))
        put(("n1g", l), fm(inp["norm1_g"][l], 8))
        put(("n2g", l), fm(inp["norm2_g"][l], 8))
        put(("mconv", l), np.ascontiguousarray(inp["m_conv"][l].reshape(9, 8, 128).transpose(2, 1, 0)))
        put(("rconv", l), np.ascontiguousarray(inp["r_conv"][l].reshape(9, 15, 128).transpose(2, 1, 0)))
        for d in range(2):
            put(("a0", l, d), fm(inp["r_a0"][l, d], 4))
        put(("kk", l), fm(inp["r_kk"][l], 4))
        put(("ka", l), fm(inp["r_ka"][l], 4))
        put(("rk", l), fm(inp["r_rk"][l], 4))
        put(("gnw", l), fm(inp["r_gn_w"][l], 4))
        put(("gnb", l), fm(inp["r_gn_b"][l], 4))
    return c


def pack_rows(inp):
    lay, n = row_layout()
    c = np.zeros((128, n), np.float32)

    def put(name, arr):
        o, w = lay[name]
        c[:, o:o + w] = np.asarray(arr, np.float32).reshape(1, w)
    for l in range(DEPTH):
        put(("gb", l), inp["m_gate_b"][l])
        put(("mng", l), inp["m_norm_g"][l])
        for d in range(2):
            put(("w0", l, d), inp["r_w0"][l, d])
    return c


def build(debug=False, nlayers=DEPTH, phases="ABCDEFG", groups="PS", nseq_p=4):
    nc = bass.Bass("TRN2", target_bir_lowering=False)
    S = Sched(nc)
    ARENA = 53200
    arena = S.sb("arena", [128, ARENA])
    PS = S.ps("ps", [128, 4096])
    aoff = [0]

    def A(shape, p0=0):
        n = int(np.prod(shape[1:]))
        o = aoff[0]
        aoff[0] += n
        assert aoff[0] <= ARENA, aoff[0]
        v = arena[p0:p0 + shape[0], o:o + n]
        if len(shape) == 3:
            v = v.rearrange("p (a b) -> p a b", b=shape[2])
        elif len(shape) == 4:
            v = v.rearrange("p (a b c) -> p a b c", b=shape[2], c=shape[3])
        return v

    def din(name, shape):
        return nc.dram_tensor(name, list(shape), F32, kind="ExternalInput").ap()

    def dout(name, shape):
        return nc.dram_tensor(name, list(shape), F32, kind="ExternalOutput").ap()

    def dscr(name, shape):
        return nc.dram_tensor(name, list(shape), F32, kind="ExternalOutput" if debug else "Internal").ap()

    def bank(b, parts=128, cols=512, c0=0):
        return PS[0:parts, b * 512 + c0:b * 512 + c0 + cols]

    def mm(out, lhsT, rhs, start, stop, reads, writes):
        S.pe(lambda e: e.matmul(out, lhsT, rhs, start=start, stop=stop), reads, writes)

    def tr(out, in_, idn, reads, writes):
        S.pe(lambda e: e.transpose(out, in_, idn), reads, writes)

    def act(out, in_, func, reads, writes, bias=None, scale=1.0):
        if bias is None:
            S.act(lambda e: e.activation(out=out, in_=in_, func=func, scale=scale), reads, writes)
        else:
            S.act(lambda e: e.activation(out=out, in_=in_, func=func, bias=bias, scale=scale), reads, writes)

    def tt(out, in0, in1, op, reads, writes, eng="dve"):
        S.add(eng, lambda e: e.tensor_tensor(out, in0, in1, op), reads, writes)

    def ts(out, in0, s1, s2, op0, op1, reads, writes, eng="dve"):
        if s2 is None:
            S.add(eng, lambda e: e.tensor_scalar(out, in0, s1, None, op0), reads, writes)
        else:
            S.add(eng, lambda e: e.tensor_scalar(out, in0, s1, s2, op0, op1), reads, writes)

    def stt(out, in0, sc, in1, op0, op1, reads, writes, eng="dve"):
        S.add(eng, lambda e: e.scalar_tensor_tensor(out, in0, sc, in1, op0, op1), reads, writes)

    def cp(out, in_, reads, writes):
        S.dve(lambda e: e.tensor_copy(out, in_), reads, writes)

    def red(out, in_, op, reads, writes):
        S.dve(lambda e: e.tensor_reduce(out, in_, AX.X, op), reads, writes)

    def rsqrt_(ap, key):
        act(ap, ap, AF.Ln, [key], [key])
        act(ap, ap, AF.Exp, [key], [key], scale=-0.5)

    def bc(ap, axis, shape):
        return ap.unsqueeze(axis).to_broadcast(list(shape))

    xin = {"P": din("xpT", [DM, NP_]), "S": din("xsT", [DM, NS_])}
    cvec_d = din("cvec", [128, 16])
    ada_w = din("ada_w", [DEPTH, DM, 6 * DM])
    w_in = din("w_in", [DEPTH, DM, PIN])
    proj_m = din("proj_m", [DEPTH, 512, DM])
    proj_r = din("proj_r", [DEPTH, 512, DM])
    w_out = din("w_out", [DEPTH, DM, DM])
    ffn_w1 = din("ffn_w1", [DEPTH, DM, FH])
    ffn_w3 = din("ffn_w3", [DEPTH, DM, FH])
    ffn_w2 = din("ffn_w2", [DEPTH, FH, DM])
    r_w2 = din("r_w2", [DEPTH, 128, 512])
    r_a2 = din("r_a2", [DEPTH, 128, 512])
    r_g2 = din("r_g2", [DEPTH, 128, 512])
    vlay, nvec = vec_layout()
    rlay, nrow = row_layout()
    clay, ncon = const_layout()
    vecs_d = din("vecs", [128, nvec])
    rows_d = din("rows", [128, nrow])
    cons_d = din("consts", [128, ncon])
    cst_d = din("cst", [DEPTH, 2, 128, 4 * 129])
    mst_d = din("mst", [DEPTH, 2, 128, 4])
    rst_d = din("rst", [DEPTH, 2, 64, 512])

    yout = {"P": dout("ypT", [DM, NP_]), "S": dout("ysT", [DM, NS_])}
    o_mc = dout("o_mc", [4, DEPTH, 2, 128, 4 * 129])
    o_mm = dout("o_mm", [4, DEPTH, 2, 4])
    o_rs = dout("o_rs", [4, DEPTH, 2, 64, 512])

    GN = {"P": NP_, "S": NS_}
    scr = {}
    for g in "PS":
        N = GN[g]
        for nm, shp in [("xT", [DM, N]), ("uqk", [1024, N]), ("ur", [1920, N]), ("mrg", [2048, N]),
                        ("uvo", [N, 1040]), ("qk", [1024, N]), ("rc", [1920, N]), ("av", [512, N]),
                        ("bonus", [512, N]), ("kt0", [512, N]), ("kt1", [512, N]), ("bv0", [512, N]),
                        ("bv1", [512, N]), ("sg0", [N, 512]), ("sg1", [N, 512]), ("vrt", [N, 512]),
                        ("hm0", [N, 512]), ("hm1", [N, 512]), ("y0", [512, N]), ("y1", [512, N])]:
            scr[g, nm] = dscr("%s_%s" % (nm, g), shp)

    cons = A([128, ncon])
    vecs = A([128, nvec])
    rows = A([128, nrow])
    cv = A([128, 16])
    modt = [A([128, 48, 2]) for _ in range(DEPTH)]
    A1t = [A([128, 8, 2]) for _ in range(DEPTH)]
    A2t = [A([128, 8, 2]) for _ in range(DEPTH)]
    omka = [A([128, 4]) for _ in range(DEPTH)]
    aw2 = A([128, 512])
    aa2 = A([128, 512])
    ag2 = A([128, 512])
    persist_end = aoff[0]

    def C(name, parts=128, c0=0, cols=None):
        o, w = clay[name]
        return cons[0:parts, o + c0:o + c0 + (w if cols is None else cols)]

    def V(name, c0=0, cols=None):
        o, w = vlay[name]
        return vecs[:, o + c0:o + c0 + (w if cols is None else cols)]

    def R(name, parts=128):
        o, w = rlay[name]
        return rows[0:parts, o:o + w]

    S.dma(cons, cons_d, ["cons_d"], ["cons"])
    S.dma(vecs, vecs_d, ["vecs_d"], ["vecs"])
    S.dma(rows, rows_d, ["rows_d"], ["rows"])
    S.dma(cv, cvec_d, ["cvec_d"], ["cv"])
    act(cv, cv, AF.Silu, ["cv"], ["cv"])
    ident = C("ident")
    ones = C("ones")
    blk = C("blk")
    id64 = C("ident", 64, 0, 64)

    bstate = {"deps": []}

    def barrier():
        last = {}
        alld = []
        for i, o in enumerate(S.ops):
            if o.is_dma:
                alld.append(i)
            else:
                last[o.eng] = i
        S.lastw["__bar__"] = None
        ids = list(last.values()) + alld[-(3 * S.n_dma_sems):]
        bstate["deps"] = ids
        aoff[0] = persist_end

    _add = S.add

    cur = {"d": 0, "vmap": None}
    _psre = re.compile(r"^ps(\d)")

    def kx(key):
        if cur["vmap"] is not None and isinstance(key, str):
            m = _psre.match(key)
            if m:
                return "ps%d" % (4 * cur["d"] + cur["vmap"][int(m.group(1))])
        return key

    def bank_(v, parts=128, cols=512, c0=0):
        return bank(4 * cur["d"] + cur["vmap"][v], parts, cols, c0)

    def drive(chains):
        active = list(chains)
        while active:
            for item in list(active):
                cur["d"] = item[0]
                try:
                    next(item[1])
                except StopIteration:
                    active.remove(item)

    def add_with_barrier(eng, fn, reads=(), writes=(), dma=False):
        reads = [kx(k_) for k_ in reads]
        writes = [kx(k_) for k_ in writes]
        i = _add(eng, fn, reads, writes, dma)
        S.ops[i].deps.update(bstate["deps"])
        return i
    S.add = add_with_barrier

    def phase_A(l):
        barrier()
        W = [A([128, 8, 512]), A([128, 8, 512])]
        aw = ada_w[l].rearrange("(kc p) n -> p kc n", p=128)
        mps = bank(0, 128, 96)
        for ng in range(12):
            w = W[ng % 2]
            wk = "W%d" % (ng % 2)
            S.dma(w, aw[:, :, ng * 512:(ng + 1) * 512], ["ada_w"], [wk])
            for nn in range(4):
                ch = ng * 4 + nn
                for kc in range(8):
                    mm(mps[:, ch * 2:ch * 2 + 2], w[:, kc, nn * 128:(nn + 1) * 128], cv[:, kc * 2:kc * 2 + 2],
                       kc == 0, kc == 7, [wk, "cv"], ["ps0"])
        mk = "mod%d" % l
        tt(modt[l], mps.rearrange("p (c j) -> p c j", j=2), bc(V(("ada_b", l)), 2, [128, 48, 2]), ALU.add,
           ["ps0", "vecs"], [mk])
        for (At, c0, gname) in ((A1t[l], 8, "n1g"), (A2t[l], 32, "n2g")):
            ts(At, modt[l][:, c0:c0 + 8, :], 1.0, None, ALU.add, None, [mk], [mk + "A"])
            tt(At, At, bc(V((gname, l)), 2, [128, 8, 2]), ALU.mult, [mk + "A", "vecs"], [mk + "A"])
        ts(omka[l], V(("ka", l)), -1.0, 1.0, ALU.mult, ALU.add, ["vecs"], ["omka"])
        if debug and l == 0:
            dm = dout("dbg_mod", [128, 96 + 16 + 16 + 16])
            S.dma(dm[:, 0:96], modt[l].rearrange("p c j -> p (c j)"), [mk], ["dbg"])
            S.dma(dm[:, 96:112], A1t[l].rearrange("p c j -> p (c j)"), [mk + "A"], ["dbg"])
            S.dma(dm[:, 112:128], A2t[l].rearrange("p c j -> p (c j)"), [mk + "A"], ["dbg"])
            S.dma(dm[:, 128:144], cv, ["cv"], ["dbg"])

    def rmsnorm_tile(xt, xk, ht, hk, sq, sqk, rt, rk_, Aap, Bap):
        act(sq, xt, AF.Square, [xk], [sqk])
        ps = bank(7)
        for c in range(8):
            mm(ps, ones, sq[:, c, :], c == 0, c == 7, ["cons", sqk], ["ps7"])
        ts(rt, ps, 1.0 / DM, 1e-6, ALU.mult, ALU.add, ["ps7"], [rk_])
        rsqrt_(rt, rk_)
        tt(ht, xt, bc(rt, 1, [128, 8, 512]), ALU.mult, [xk, rk_], [hk])
        for c in range(8):
            if Bap is None:
                ts(ht[:, c, :], ht[:, c, :], Aap(c), None, ALU.mult, None, [hk, "vecs", "mod"], [hk])
            else:
                ts(ht[:, c, :], ht[:, c, :], Aap(c), Bap(c), ALU.mult, ALU.add, [hk, "vecs", "mod"], [hk])

    def phase_B(l, g, j):
        barrier()
        N = GN[g]
        xsrc = (xin[g] if l == 0 else scr[g, "xT"]).rearrange("(c p) t -> p c t", p=128)
        xt = A([128, 8, 512]); ht = A([128, 8, 512]); sq = A([128, 8, 512]); rt = A([128, 512])
        W = [A([128, 8, 512]), A([128, 8, 512])]
        OUT = [A([128, 4, 512]), A([128, 4, 512])]
        win = w_in[l].rearrange("(kc p) n -> p kc n", p=128)
        wcnt = [0]
        ocnt = [0]
        pcnt = [0]
        for ti in range(N // NT):
            tsl = slice(ti * NT, (ti + 1) * NT)
            S.dma(xt, xsrc[:, :, tsl], ["x_" + g], ["xt"])
            rmsnorm_tile(xt, "xt", ht, "ht", sq, "sq", rt, "rt",
                         lambda c: A1t[l][:, c, j:j + 1], lambda c: modt[l][:, c, j:j + 1])
            for (nm, col0, nch, fn) in (("uqk", 0, 8, AF.Copy), ("ur", 2064, 15, AF.Copy), ("mrg", 3984, 16, AF.Sigmoid)):
                dst = scr[g, nm].rearrange("(c p) t -> p c t", p=128)
                for cg in range(0, nch, 4):
                    ncc = min(4, nch - cg)
                    wi_ = wcnt[0] % 2; wcnt[0] += 1
                    w = W[wi_]; wk = "W%d" % wi_
                    S.dma(w[:, :, 0:ncc * 128], win[:, :, col0 + cg * 128:col0 + (cg + ncc) * 128], ["w_in"], [wk])
                    oi = ocnt[0] % 2; ocnt[0] += 1
                    o = OUT[oi]; ok = "OUT%d" % oi
                    for cc in range(ncc):
                        pb = pcnt[0] % 4; pcnt[0] += 1
                        ps = bank(pb)
                        for kc in range(8):
                            mm(ps, w[:, kc, cc * 128:(cc + 1) * 128], ht[:, kc, :], kc == 0, kc == 7,
                               [wk, "ht"], ["ps%d" % pb])
                        act(o[:, cc, :], ps, fn, ["ps%d" % pb], [ok])
                    S.dma(dst[:, cg:cg + ncc, tsl], o[:, 0:ncc, :], [ok], [nm + g], eng=STORE_Q)
            for (c0, ncol) in ((1024, 512), (1536, 512), (2048, 16)):
                wi_ = wcnt[0] % 2; wcnt[0] += 1
                w = W[wi_]; wk = "W%d" % wi_
                S.dma(w[:, :, 0:ncol], win[:, :, c0:c0 + ncol], ["w_in"], [wk])
                oi = ocnt[0] % 2; ocnt[0] += 1
                o = OUT[oi]; ok = "OUT%d" % oi
                for t4 in range(4):
                    pb = pcnt[0] % 4; pcnt[0] += 1
                    ps = bank(pb, 128, ncol)
                    for kc in range(8):
                        mm(ps, ht[:, kc, t4 * 128:(t4 + 1) * 128], w[:, kc, 0:ncol], kc == 0, kc == 7,
                           [wk, "ht"], ["ps%d" % pb])
                    act(o[:, t4, 0:ncol], ps, AF.Copy, ["ps%d" % pb], [ok])
                dstv = scr[g, "uvo"][tsl, c0 - 1024:c0 - 1024 + ncol].rearrange("(t p) n -> p t n", p=128)
                S.dma(dstv, o[:, :, 0:ncol], [ok], ["uvo" + g], eng=STORE_Q)

    def phase_C(l, g):
        barrier()
        N = GN[g]
        bufs = [(A([128, 4096]), A([128, 4096])), (A([128, 4096]), A([128, 4096]))]
        if g == "S":
            Rr, Wd = 64, 64
            taps = [(dy, dx) for dy in range(3) for dx in range(3)]
        else:
            Rr, Wd = 4, 256
            taps = [(1, 0), (1, 1), (1, 2)]
        k = 0
        for (src, dstn, nch, vname) in (("uqk", "qk", 8, "mconv"), ("ur", "rc", 15, "rconv")):
            for c in range(nch):
                cin_, acc_ = bufs[k % 2]
                ik, ak = "cin%d" % (k % 2), "acc%d" % (k % 2)
                k += 1
                cin = cin_[:, 0:N]; acc = acc_[:, 0:N]
                S.dma(cin, scr[g, src][c * 128:(c + 1) * 128, :], [src + g], [ik])
                c3 = cin.rearrange("p (r w) -> p r w", w=Wd)
                a3 = acc.rearrange("p (r w) -> p r w", w=Wd)
                wv = lambda dy, dx: V((vname, l), c * 9 + dy * 3 + dx, 1)
                ts(acc, cin, wv(1, 1), None, ALU.mult, None, [ik, "vecs"], [ak])
                ti_ = 0
                for (dy, dx) in taps:
                    if (dy, dx) == (1, 1):
                        continue
                    oy, ox = dy - 1, dx - 1
                    r0, r1 = max(0, -oy), Rr - max(0, oy)
                    q0, q1 = max(0, -ox), Wd - max(0, ox)
                    eng = "dve"
                    ti_ += 1
                    stt(a3[:, r0:r1, q0:q1], c3[:, r0 + oy:r1 + oy, q0 + ox:q1 + ox], wv(dy, dx), a3[:, r0:r1, q0:q1],
                        ALU.mult, ALU.add, [ik, ak, "vecs"], [ak], eng=eng)
                if src == "uqk":
                    act(acc, acc, AF.Silu, [ak], [ak])
                    if c < 4:
                        ts(acc, acc, 128.0 ** -0.5, None, ALU.mult, None, [ak], [ak])
                elif c == 12:
                    act(acc, acc, AF.Tanh, [ak], [ak])
                elif c == 14:
                    act(acc, acc, AF.Sigmoid, [ak], [ak])
                S.dma(scr[g, dstn][c * 128:(c + 1) * 128, :], acc, [ak], [dstn + g], eng=STORE_Q)

    def phase_D(l, g):
        barrier()
        N = GN[g]
        S.dma(aw2, r_w2[l], ["r_w2"], ["aw2"])
        S.dma(aa2, r_a2[l], ["r_a2"], ["aa2"])
        RK = A([128, 8, 512]); VV = A([128, 6, 512]); T2 = A([128, 8, 512]); T3 = A([128, 8, 512])
        STG = A([128, 4, 512])
        rcv = scr[g, "rc"].rearrange("(c p) t -> p c t", p=128)
        fmv = lambda nm: scr[g, nm].rearrange("(c p) t -> p c t", p=128)
        tmv = lambda nm, tsl: scr[g, nm][tsl, :].rearrange("(t p) n -> p t n", p=128)
        for ti in range(N // NT):
            tsl = slice(ti * NT, (ti + 1) * NT)
            S.dma(RK, rcv[:, 0:8, tsl], ["rc" + g], ["RK"])
            S.dma(VV, rcv[:, 8:14, tsl], ["rc" + g], ["VV"])
            for c in range(4):
                ts(T2[:, c, :], RK[:, 4 + c, :], V(("kk", l), c, 1), None, ALU.mult, None, ["RK", "vecs"], ["T2a"])
            tt(T2[:, 4:8, :], T2[:, 0:4, :], T2[:, 0:4, :], ALU.mult, ["T2a"], ["T2b"])
            for c in range(4):
                mm(bank(c), blk, T2[:, 4 + c, :], True, True, ["cons", "T2b"], ["ps%d" % c])
                ts(T3[:, c, :], bank(c), 1e-24, None, ALU.max, None, ["ps%d" % c], ["T3a"])
            rsqrt_(T3[:, 0:4, :], "T3a")
            stt(T3[:, 4:8, :], T2[:, 0:4, :], -1.0, T3[:, 0:4, :], ALU.mult, ALU.mult, ["T2a", "T3a"], ["T3b"])
            S.dma(fmv("av")[:, :, tsl], T3[:, 4:8, :], ["T3b"], ["av" + g], eng=STORE_Q)
            for c in range(4):
                stt(T2[:, 4 + c, :], RK[:, c, :], V(("rk", l), c, 1), RK[:, 4 + c, :], ALU.mult, ALU.mult,
                    ["RK", "vecs"], ["T2b"])
                mm(bank(4 + c), blk, T2[:, 4 + c, :], True, True, ["cons", "T2b"], ["ps%d" % (4 + c)])
                tt(T3[:, c, :], bank(4 + c), VV[:, c, :], ALU.mult, ["ps%d" % (4 + c), "VV"], ["T3a"])
            S.dma(fmv("bonus")[:, :, tsl], T3[:, 0:4, :], ["T3a"], ["bonus" + g], eng=STORE_Q)
            for d in range(2):
                pb_ = 64 * d
                for c in range(4):
                    mm(bank(c), aa2[pb_:pb_ + 64, c * 128:(c + 1) * 128], VV[pb_:pb_ + 64, 5, :], True, True,
                       ["aa2", "VV"], ["ps%d" % c])
                    act(T2[:, c, :], bank(c), AF.Sigmoid, ["ps%d" % c, "vecs"], ["T2a"], bias=V(("a0", l, d), c, 1))
                    ts(T2[:, 4 + c, :], T2[:, c, :], V(("ka", l), c, 1), omka[l][:, c:c + 1], ALU.mult, ALU.add,
                       ["T2a", "vecs", "omka"], ["T2b"])
                tt(T2[:, 4:8, :], T2[:, 4:8, :], RK[:, 4:8, :], ALU.mult, ["T2b", "RK"], ["T2b"])
                S.dma(fmv("kt%d" % d)[:, :, tsl], T2[:, 4:8, :], ["T2b"], ["kt%d" % d + g], eng=STORE_Q)
                stt(T3[:, 0:4, :], T3[:, 4:8, :], -1.0, T2[:, 0:4, :], ALU.mult, ALU.mult, ["T3b", "T2a"], ["T3a"])
                S.dma(fmv("bv%d" % d)[:, :, tsl], T3[:, 0:4, :], ["T3a"], ["bv%d" % d + g], eng=STORE_Q)
                for t4 in range(4):
                    ps = bank(4 + t4)
                    mm(ps, VV[pb_:pb_ + 64, 4, t4 * 128:(t4 + 1) * 128], aw2[pb_:pb_ + 64, :], True, False,
                       ["VV", "aw2"], ["ps%d" % (4 + t4)])
                    mm(ps, ones[0:1, :], R(("w0", l, d), 1), False, True, ["cons", "rows"], ["ps%d" % (4 + t4)])
                    act(STG[:, t4, :], ps, AF.Sigmoid, ["ps%d" % (4 + t4)], ["STG"])
                S.dma(tmv("sg%d" % d, tsl), STG, ["STG"], ["sg%d" % d + g], eng=STORE_Q)
            for t4 in range(4):
                ps = bank(t4)
                for c in range(4):
                    mm(ps[:, c * 128:(c + 1) * 128], VV[:, c, t4 * 128:(t4 + 1) * 128], ident, True, True,
                       ["VV", "cons"], ["ps%d" % t4])
                act(STG[:, t4, :], ps, AF.Copy, ["ps%d" % t4], ["STG"])
            S.dma(tmv("vrt", tsl), STG, ["STG"], ["vrt" + g], eng=STORE_Q)

    def phase_E(l, g):
        barrier()
        N = GN[g]
        nseq, T = (nseq_p, 256) if g == "P" else (1, 4096)
        nch = T // 64
        qkv = scr[g, "qk"].rearrange("(c p) t -> p c t", p=128)
        uvo = scr[g, "uvo"]
        D = []
        for d in range(2):
            t = dict(QK=A([128, 8, 64]), VT=A([64, 4, 129]), GT=A([64, 16]), CS=A([128, 4, 129]), MR=A([128, 4]),
                     e1=A([64, 4]), nl=A([64, 4]), nbs=A([64, 4]), gg=A([64, 4]), dg=A([64, 4, 64]), cm=A([64, 4]),
                     gmx=A([128, 4]), aa=A([64, 4]), wi=A([64, 4]), emt=A([64, 4]), da=A([64, 4, 64]),
                     wt=A([64, 4, 64]), swt=A([64, 4, 64]), r1=A([64, 4, 129]), dn=A([64, 4]), hout=A([64, 4, 128]),
                     mm_=A([128, 4]), wie=A([128, 4]), we=A([64, 4]), kws=A([64, 4, 128]), nbl=A([128, 4]))
            D.append(t)
            S.add("dve", lambda e, ap=t["VT"]: e.memset(ap, 1.0), [], ["VT%d" % d])

        def chunk(d, seq, ci, first, last):
            t = D[d]
            k = lambda nm: nm + str(d)
            tok = seq * T + ci * 64
            if first:
                if g == "P":
                    S.add("dve", lambda e: e.memset(t["CS"], 0.0), [], [k("CS")])
                    S.add("dve", lambda e: e.memset(t["MR"], 0.0), [], [k("MR")])
                else:
                    S.dma(t["CS"], cst_d[l, d].rearrange("p (h e) -> p h e", e=129), ["cst"], [k("CS")])
                    S.dma(t["MR"], mst_d[l, d], ["mst"], [k("MR")])
            S.dma(t["QK"], qkv[:, :, tok:tok + 64], ["qk" + g], [k("QK")])
            S.dma(t["VT"][:, :, 0:128], uvo[tok:tok + 64, 0:512].rearrange("p (h e) -> p h e", e=128),
                  ["uvo" + g], [k("VT")])
            S.dma(t["GT"], uvo[tok:tok + 64, 1024:1040], ["uvo" + g], [k("GT")])
            tt(t["GT"], t["GT"], R(("gb", l), 64), ALU.add, [k("GT"), "rows"], [k("GT")])
            ic, fc = 8 * d, 8 * d + 4
            act(t["e1"], t["GT"][:, fc:fc + 4], AF.Exp, [k("GT")], [k("e1")], scale=-1.0)
            ts(t["e1"], t["e1"], 1.0, None, ALU.add, None, [k("e1")], [k("e1")])
            act(t["nl"], t["e1"], AF.Ln, [k("e1")], [k("nl")])
            yield
            mm(bank_(0, 64, 4), C(("tri", d), 64), t["nl"], True, True, ["cons", k("nl")], ["ps0a"])
            yield
            mm(bank_(2, 128, 4, 384), ones[0:64, :], t["nl"], True, True, ["cons", k("nl")], ["ps2b"])
            tt(t["gg"], t["GT"][:, ic:ic + 4], bank_(0, 64, 4), ALU.add, [k("GT"), "ps0a", "ps2b"], [k("gg")])
            cp(t["nbs"], bank_(0, 64, 4), ["ps0a", "ps2b"], [k("nbs")])
            cp(t["nbl"], bank_(2, 128, 4, 384), ["ps2b", "ps0a"], [k("nbl")])
            tt(t["dg"], bc(id64, 1, [64, 4, 64]), bc(t["gg"], 2, [64, 4, 64]), ALU.mult, ["cons", k("gg")], [k("dg")])
            dgf = t["dg"].rearrange("p h s -> p (h s)")
            psG = bank_(0, 64, 256, 256)
            yield
            mm(psG, ones[0:64, 0:64], dgf, True, False, ["cons", k("dg")], ["ps0c"])
            yield
            mm(psG, id64, C(("mneg", d), 64), False, True, ["cons"], ["ps0c"])
            red(t["cm"], psG.rearrange("p (h s) -> p h s", s=64), ALU.max, ["ps0c"], [k("cm")])
            psM = bank_(1, 128, 256)
            yield
            mm(psM, ones[0:64, :], dgf, True, True, ["cons", k("dg")], ["ps1a"])
            red(t["gmx"], psM.rearrange("p (h s) -> p h s", s=64), ALU.max, ["ps1a"], [k("gmx")])
            tt(t["aa"], t["cm"], t["MR"][0:64, :], ALU.max, [k("cm"), k("MR")], [k("aa")])
            tt(t["wi"], t["MR"][0:64, :], t["aa"], ALU.subtract, [k("MR"), k("aa")], [k("wi")])
            act(t["wi"], t["wi"], AF.Exp, [k("wi")], [k("wi")])
            tt(t["emt"], t["nbs"], t["aa"], ALU.subtract, [k("nbs"), k("aa")], [k("emt")])
            act(t["emt"], t["emt"], AF.Exp, [k("emt")], [k("emt")])
            tt(t["da"], bc(id64, 1, [64, 4, 64]), bc(t["aa"], 2, [64, 4, 64]), ALU.mult, ["cons", k("aa")], [k("da")])
            psA = bank_(1, 64, 256, 256)
            yield
            mm(psA, ones[0:64, 0:64], t["da"].rearrange("p h s -> p (h s)"), True, False, ["cons", k("da")], ["ps1b"])
            yield
            mm(psA, id64, C(("mpos", d), 64), False, True, ["cons"], ["ps1b"])
            tt(t["wt"], psA.rearrange("p (h s) -> p h s", s=64), bc(t["gg"], 2, [64, 4, 64]), ALU.subtract,
               ["ps1b", k("gg")], [k("wt")])
            act(t["wt"], t["wt"], AF.Exp, [k("wt")], [k("wt")], scale=-1.0)
            psS = bank_(2, 64, 256)
            yield
            for h in range(4):
                mm(psS[:, h * 64:(h + 1) * 64], t["QK"][:, 4 + h, :], t["QK"][:, h, :], True, True, [k("QK")], ["ps2a"])
            tt(t["swt"], psS.rearrange("p (h s) -> p h s", s=64), t["wt"], ALU.mult, ["ps2a", k("wt")], [k("swt")])
            yield
            for h in range(4):
                bI = 4 + h // 2
                mm(bank_(bI, 64, 129, (h % 2) * 256), t["swt"][:, h, :], t["VT"][:, h, :], True, True,
                   [k("swt"), k("VT")], ["ps%d" % bI])
                bN = 6 + h // 2
                mm(bank_(bN, 64, 129, (h % 2) * 256), t["QK"][:, h, :], t["CS"][:, h, :], True, True,
                   [k("QK"), k("CS")], ["ps%d" % bN])
            for hb in range(2):
                vN = bank_(6 + hb, 64).rearrange("p (h e) -> p h e", e=256)[:, :, 0:129]
                vI = bank_(4 + hb, 64).rearrange("p (h e) -> p h e", e=256)[:, :, 0:129]
                r1v = t["r1"][:, 2 * hb:2 * hb + 2, :]
                tt(r1v, vN, bc(t["wi"][:, 2 * hb:2 * hb + 2], 2, [64, 2, 129]), ALU.mult,
                   ["ps%d" % (6 + hb), k("wi")], [k("r1")])
                tt(r1v, r1v, vI, ALU.add, [k("r1"), "ps%d" % (4 + hb)], [k("r1")])
            ts(t["dn"], t["r1"][:, :, 128], -1.0, None, ALU.mult, None, [k("r1")], [k("dn")])
            tt(t["dn"], t["dn"], t["r1"][:, :, 128], ALU.max, [k("dn"), k("r1")], [k("dn")])
            tt(t["dn"], t["dn"], t["emt"], ALU.max, [k("dn"), k("emt")], [k("dn")])
            S.dve(lambda e: e.reciprocal(t["dn"], t["dn"]), [k("dn")], [k("dn")])
            tt(t["hout"], t["r1"][:, :, 0:128], bc(t["dn"], 2, [64, 4, 128]), ALU.mult, [k("r1"), k("dn")], [k("hout")])
            S.dma(scr[g, "hm%d" % d][tok:tok + 64, :], t["hout"].rearrange("p h e -> p (h e)"), [k("hout")],
                  ["hm%d" % d + g], eng=STORE_Q)
            if debug and g == "P" and d == 0 and seq == 1 and ci == 0 and l == 0:
                dd = dout("dbg_e", [64, 64])
                for i_, nm_ in enumerate(("nbs", "gg", "cm", "aa", "emt", "wi", "dn", "nl", "e1")):
                    S.dma(dd[:, 4 * i_:4 * i_ + 4], t[nm_], [k(nm_)], ["dbg_e"])
                S.dma(dd[:, 40:56], t["GT"], [k("GT")], ["dbg_e"])
                S.dma(dd[:, 56:60], t["r1"][:, :, 128], [k("r1")], ["dbg_e"], allow_slow_non_contiguous=True)
            tt(t["mm_"], t["MR"], t["gmx"], ALU.max, [k("MR"), k("gmx")], [k("mm_")])
            tt(t["wie"], t["MR"], t["mm_"], ALU.subtract, [k("MR"), k("mm_")], [k("wie")])
            act(t["wie"], t["wie"], AF.Exp, [k("wie")], [k("wie")])
            tt(t["we"], t["gg"], t["mm_"][0:64, :], ALU.subtract, [k("gg"), k("mm_")], [k("we")])
            act(t["we"], t["we"], AF.Exp, [k("we")], [k("we")])
            tt(t["MR"], t["mm_"], t["nbl"], ALU.subtract, [k("mm_"), k("nbl"), k("wi"), k("wie")], [k("MR")])
            psK = bank_(3, 64, 512)
            yield
            for h in range(4):
                tr(psK[:, h * 128:(h + 1) * 128], t["QK"][:, 4 + h, :], ident, [k("QK"), "cons"], ["ps3"])
            tt(t["kws"], psK.rearrange("p (h e) -> p h e", e=128), bc(t["we"], 2, [64, 4, 128]), ALU.mult,
               ["ps3", k("we")], [k("kws")])
            yield
            for h in range(4):
                bI = 4 + h // 2
                mm(bank_(bI, 128, 129, (h % 2) * 256), t["kws"][:, h, :], t["VT"][:, h, :], True, True,
                   [k("kws"), k("VT")], ["ps%d" % bI])
            tt(t["CS"], t["CS"], bc(t["wie"], 2, [128, 4, 129]), ALU.mult, [k("CS"), k("wie")], [k("CS")])
            for hb in range(2):
                vD = bank_(4 + hb).rearrange("p (h e) -> p h e", e=256)[:, :, 0:129]
                csv = t["CS"][:, 2 * hb:2 * hb + 2, :]
                tt(csv, csv, vD, ALU.add, [k("CS"), "ps%d" % (4 + hb)], [k("CS")])
            if last and g == "P":
                S.dma(o_mc[seq, l, d], t["CS"].rearrange("p h e -> p (h e)"), [k("CS")], ["o_mc"], eng=STORE_Q)
                S.dma(o_mm[seq, l, d:d + 1, :], t["MR"][0:1, :], [k("MR")], ["o_mm"], eng=STORE_Q)

        def chain(d, seq):
            for i in range(nch):
                yield from chunk(d, seq, i if d == 0 else nch - 1 - i, i == 0, i == nch - 1)

        cur["vmap"] = {0: 0, 1: 1, 2: 2, 3: 3, 4: 3, 5: 0, 6: 1, 7: 2}
        for seq in range(nseq):
            drive([(0, chain(0, seq)), (1, chain(1, seq))])
        cur["vmap"] = None

    def phase_F(l, g):
        barrier()
        N = GN[g]
        nseq, T = (nseq_p, 256) if g == "P" else (1, 4096)
        nch = T // 64
        hv = lambda nm: scr[g, nm].rearrange("(h k) t -> k h t", k=64)
        D = []
        for d in range(2):
            t = dict(FM=A([64, 4, 8, 64]), SG=A([64, 512]), UV=A([128, 8, 64]), ST=A([64, 8, 64]),
                     ecl=A([64, 8, 64]), encl=A([64, 8, 64]), eclx=A([64, 8, 64]), RT=A([64, 8, 128]),
                     LT=A([64, 8, 128]), M1=A([128, 8, 128]), Q=[A([64, 8, 64]), A([64, 8, 64])],
                     P=[A([64, 8, 64]), A([64, 8, 64])], Rm=[A([64, 8, 64]), A([64, 8, 64])],
                     LTk=A([128, 8, 64]), Xs=A([64, 8, 64]), Ys=A([64, 8, 64]), AK=A([64, 8, 64]), VL=A([64, 8, 64]))
            D.append(t)

        def b3(b, parts, inner):
            return bank_(b, parts).rearrange("p (h s) -> p h s", s=inner)

        def chunk(d, seq, ci, first, last):
            t = D[d]
            k = lambda nm: nm + str(d)
            tok = seq * T + ci * 64
            tsl = slice(tok, tok + 64)
            if first:
                if g == "P":
                    S.add("dve", lambda e: e.memset(t["ST"], 0.0), [], [k("ST")])
                else:
                    S.dma(t["ST"], rst_d[l, d].rearrange("p (h v) -> p h v", v=64), ["rst"], [k("ST")])
            for i, nm in enumerate(("rc", "av", "kt%d" % d, "bv%d" % d)):
                src = hv(nm)[:, 0:8, tsl]
                S.dma(t["FM"][:, i, :, :], src, [nm + g], [k("FM")])
            S.dma(t["SG"], scr[g, "sg%d" % d][tsl, :], ["sg%d" % d + g], [k("SG")])
            S.dma(t["UV"][64:128, :, :], scr[g, "vrt"][tsl, :].rearrange("p (h v) -> p h v", v=64), ["vrt" + g], [k("UV")])
            S.dma(t["VL"], scr[g, "vrt"][tsl, :].rearrange("p (h v) -> p h v", v=64), ["vrt" + g], [k("VL")])
            if FCUT < 2:
                return
            yield
            for h in range(8):
                mm(bank_(h // 4, 64, 128, (h % 4) * 128), t["SG"][:, h * 64:(h + 1) * 64], C(("tri2", d), 64), True, True,
                   [k("SG"), "cons"], ["ps%d" % (h // 4)])
            for hb in range(2):
                cv_ = b3(hb, 64, 128)
                hs = slice(4 * hb, 4 * hb + 4)
                act(t["ecl"][:, hs, :], cv_[:, :, 0:64], AF.Exp, ["ps%d" % hb], [k("ecl")])
                act(t["encl"][:, hs, :], cv_[:, :, 0:64], AF.Exp, ["ps%d" % hb], [k("encl")], scale=-1.0)
                act(t["eclx"][:, hs, :], cv_[:, :, 64:128], AF.Exp, ["ps%d" % hb], [k("eclx")])
            if FCUT < 3:
                return
            FMr, FMa, FMk, FMb = (t["FM"][:, i, :, :] for i in range(4))
            tt(t["RT"][:, :, 0:64], FMa, t["eclx"], ALU.mult, [k("FM"), k("eclx")], [k("RT")])
            tt(t["RT"][:, :, 64:128], FMr, t["ecl"], ALU.mult, [k("FM"), k("ecl")], [k("RT")])
            tt(t["LT"][:, :, 0:64], FMb, t["encl"], ALU.mult, [k("FM"), k("encl")], [k("LT")])
            tt(t["LT"][:, :, 64:128], FMk, t["encl"], ALU.mult, [k("FM"), k("encl")], [k("LT")])
            if FCUT < 4:
                return
            yield
            for h in range(8):
                mm(bank_(2 + h // 4, 128, 128, (h % 4) * 128), t["LT"][:, h, :], t["RT"][:, h, :], True, True,
                   [k("LT"), k("RT")], ["ps%d" % (2 + h // 4)])
            for hb in range(2):
                tt(t["M1"][:, 4 * hb:4 * hb + 4, :], b3(2 + hb, 128, 128), C(("m1mask", d)).rearrange("p (h s) -> p h s", s=128),
                   ALU.mult, ["ps%d" % (2 + hb), "cons"], [k("M1")])
            if FCUT < 5:
                return
            yield
            for h in range(8):
                mm(bank_(4, 64, 64, h * 64), t["RT"][:, h, 0:64], t["LT"][:, h, 0:64], True, True, [k("LT"), k("RT")], ["ps4"])
            Qc, Pc, Rc = t["Q"][0], t["M1"][0:64, :, 0:64], t["Rm"][0]
            tt(Qc, b3(4, 64, 64), C(("qmask", d), 64).rearrange("p (h s) -> p h s", s=64), ALU.mult, ["ps4", "cons"], [k("Q0")])
            tt(Rc, Pc, bc(id64, 1, [64, 8, 64]), ALU.add, [k("M1"), "cons"], [k("R0")])
            qk_, pk_, rk_ = k("Q0"), k("M1"), k("R0")
            if FCUT < 6:
                return
            yield
            for j in range(1, 6):
                Qn, Pn, Rn = t["Q"][j % 2], t["P"][j % 2], t["Rm"][j % 2]
                qn_, pn_, rn_ = k("Q%d" % (j % 2)), k("P%d" % (j % 2)), k("R%d" % (j % 2))
                yield
                for h in range(8):
                    mm(bank_(6, 64, 64, h * 64), Pc[:, h, :], Qc[:, h, :], True, True, [pk_, qk_], ["ps6"])
                if j < 5:
                    for h in range(8):
                        mm(bank_(5, 64, 64, h * 64), Qc[:, h, :], Pc[:, h, :], True, True, [pk_, qk_], ["ps5"])
                    act(Pn, b3(5, 64, 64), AF.Copy, ["ps5"], [pn_])
                cp(Qn, b3(6, 64, 64), ["ps6"], [qn_])
                yield
                for h in range(8):
                    mm(bank_(7, 64, 64, h * 64), Qn[:, h, :], Rc[:, h, :], True, True, [qn_, rk_], ["ps7"])
                tt(Rn, Rc, b3(7, 64, 64), ALU.add, [rk_, "ps7"], [rn_])
                Qc, Pc, Rc, qk_, pk_, rk_ = Qn, Pn, Rn, qn_, pn_, rn_
            if FCUT < 7:
                return
            yield
            for h in range(8):
                tr(bank_(4, 128, 64, h * 64), t["LT"][:, h, :], id64, [k("LT"), "cons"], ["ps4"])
            act(t["LTk"], b3(4, 128, 64), AF.Copy, ["ps4"], [k("LTk")])
            if FCUT < 8:
                return
            yield
            for h in range(8):
                mm(bank_(6, 64, 64, h * 64), t["LT"][:, h, 64:128], t["RT"][:, h, 0:64], True, True, [k("LT"), k("RT")], ["ps6"])
            tt(t["AK"], b3(6, 64, 64),
               C(("akmask", d), 64).rearrange("p (h s) -> p h s", s=64), ALU.mult, ["ps6", "cons"], [k("AK")])
            yield
            for h in range(8):
                o_ = bank_(5, 64, 64, h * 64)
                mm(o_, t["RT"][:, h, 0:64], t["ST"][:, h, :], True, False, [k("RT"), k("ST")], ["ps5"])
                mm(o_, t["AK"][:, h, :], t["VL"][:, h, :], False, True, [k("AK"), k("VL")], ["ps5"])
            act(t["Xs"], b3(5, 64, 64), AF.Copy, ["ps5"], [k("Xs")])
            yield
            for h in range(8):
                mm(bank_(6, 64, 64, h * 64), Rc[:, h, :], t["Xs"][:, h, :], True, True, [rk_, k("Xs")], ["ps6"])
            cp(t["UV"][0:64, :, :], b3(6, 64, 64), ["ps6"], [k("UV")])
            if FCUT < 9:
                return
            yield
            for h in range(8):
                o_ = bank_(7, 64, 64, h * 64)
                mm(o_, t["ST"][:, h, :], t["RT"][:, h, 64:128], True, False, [k("RT"), k("ST")], ["ps7"])
                mm(o_, t["UV"][:, h, :], t["M1"][:, h, 64:128], False, True, [k("M1"), k("UV")], ["ps7"])
            act(t["Ys"], b3(7, 64, 64), AF.Copy, ["ps7"], [k("Ys")])
            S.dma(hv("y%d" % d)[:, :, tsl], t["Ys"], [k("Ys")], ["y%d" % d + g], eng=STORE_Q)
            if FCUT < 10:
                return
            yield
            for h in range(8):
                mm(bank_(5, 64, 64, h * 64), t["LTk"][:, h, :], t["UV"][:, h, :], True, True, [k("LTk"), k("UV")], ["ps5"])
            tt(t["ST"], t["ST"], b3(5, 64, 64), ALU.add, [k("ST"), "ps5"], [k("ST")])
            li = 63 if d == 0 else 0
            tt(t["ST"], t["ST"], t["ecl"][:, :, li:li + 1].to_broadcast([64, 8, 64]), ALU.mult, [k("ST"), k("ecl")], [k("ST")])
            if last and g == "P":
                S.dma(o_rs[seq, l, d], t["ST"].rearrange("p h v -> p (h v)"), [k("ST")], ["o_rs"], eng=STORE_Q)

        def chain(d, seq):
            for i in range(nch):
                yield from chunk(d, seq, i if d == 0 else nch - 1 - i, i == 0, i == nch - 1)

        cur["vmap"] = {0: 0, 1: 1, 2: 2, 3: 3, 4: 0, 5: 1, 6: 2, 7: 3}
        for seq in range(nseq):
            drive([(0, chain(0, seq)), (1, chain(1, seq))])
        cur["vmap"] = None

    def phase_G(l, g, j, is_last):
        barrier()
        N = GN[g]
        S.dma(ag2, r_g2[l], ["r_g2"], ["ag2"])
        xsrc = (xin[g] if l == 0 else scr[g, "xT"]).rearrange("(c p) t -> p c t", p=128)
        xt = A([128, 8, 512]); mg = A([128, 8, 512]); t2 = A([128, 8, 512]); t3 = A([128, 8, 512]); rt = A([128, 512])
        W = [A([128, 8, 512]), A([128, 8, 512])]
        HID = A([128, 22, 512])
        WF = [A([128, 22, 128]), A([128, 22, 128])]
        HF = A([128, 512]); HB = t3[:, 0, :]; OO = rt; ss = A([128, 4])
        HMT = t2[:, 0:4, :]
        HR = t2[:, 4:8, :]
        fmv = lambda nm: scr[g, nm].rearrange("(c p) t -> p c t", p=128)
        wcnt = [0]; pcnt = [0]

        def nextW():
            i = wcnt[0] % 2; wcnt[0] += 1
            return W[i], "W%d" % i

        def nextP():
            b = pcnt[0] % 6; pcnt[0] += 1
            return bank(b), "ps%d" % b

        for ti in range(N // NT):
            tsl = slice(ti * NT, (ti + 1) * NT)
            S.dma(xt, xsrc[:, :, tsl], ["x_" + g], ["xt"])
            for t4 in range(4):
                tk = slice(ti * NT + t4 * 128, ti * NT + (t4 + 1) * 128)
                S.dma(HF, scr[g, "hm0"][tk, :], ["hm0" + g], ["HF"])
                S.dma(HB, scr[g, "hm1"][tk, :], ["hm1" + g], ["Y0"])
                S.dma(OO, scr[g, "uvo"][tk, 512:1024], ["uvo" + g], ["rt"])
                tt(HF, HF, HB, ALU.add, ["HF", "Y0"], ["HF"])
                tt(HB, HF, HF, ALU.mult, ["HF"], ["Y0"])
                red(ss, HB.rearrange("p (h e) -> p h e", e=128), ALU.add, ["Y0"], ["ss"])
                ts(ss, ss, 1.0 / 128, 1e-6, ALU.mult, ALU.add, ["ss"], ["ss"])
                rsqrt_(ss, "ss")
                hf3 = HF.rearrange("p (h e) -> p h e", e=128)
                tt(hf3, hf3, bc(ss, 2, [128, 4, 128]), ALU.mult, ["HF", "ss"], ["HF"])
                tt(HF, HF, R(("mng", l)), ALU.mult, ["HF", "rows"], ["HF"])
                act(OO, OO, AF.Sigmoid, ["rt"], ["rt"])
                tt(HF, HF, OO, ALU.mult, ["HF", "rt"], ["HF"])
                ps, pk = nextP()
                for c in range(4):
                    mm(ps[:, c * 128:(c + 1) * 128], HF[:, c * 128:(c + 1) * 128], ident, True, True, ["HF", "cons"], [pk])
                act(HMT[:, :, t4 * 128:(t4 + 1) * 128], ps.rearrange("p (c t) -> p c t", t=128), AF.Copy, [pk], ["HMT"])
            S.dma(mg, fmv("mrg")[:, 0:8, tsl], ["mrg" + g], ["mg"])
            w, wk = nextW()
            S.dma(w[:, 0:4, :],
                  proj_m[l].rearrange("(kc p) n -> p kc n", p=128)[:, :, 0:512], ["proj_m"], [wk])
            S.dma(w[:, 4:8, :], proj_m[l].rearrange("(kc p) n -> p kc n", p=128)[:, :, 512:1024], ["proj_m"], [wk])
            for oc in range(8):
                ps, pk = nextP()
                wv = w[:, 0:4, :] if oc < 4 else w[:, 4:8, :]
                for kc in range(4):
                    mm(ps, wv[:, kc, (oc % 4) * 128:(oc % 4 + 1) * 128], HMT[:, kc, :], kc == 0, kc == 3, [wk, "HMT"], [pk])
                tt(mg[:, oc, :], mg[:, oc, :], ps, ALU.mult, ["mg", pk], ["mg"])
            Y0 = t3[:, 0:4, :]; Y1 = t3[:, 4:8, :]
            S.dma(Y0, fmv("y0")[:, :, tsl], ["y0" + g], ["Y0"])
            S.dma(Y1, fmv("y1")[:, :, tsl], ["y1" + g], ["Y1"])
            tt(Y0, Y0, Y1, ALU.add, ["Y0", "Y1"], ["Y0"])
            for c in range(4):
                ps, pk = nextP()
                mm(ps, blk, Y0[:, c, :], True, True, ["cons", "Y0"], [pk])
                stt(Y0[:, c, :], ps, -1.0 / 64, Y0[:, c, :], ALU.mult, ALU.add, [pk, "Y0"], ["Y0"])
            tt(Y1, Y0, Y0, ALU.mult, ["Y0"], ["Y1"])
            for c in range(4):
                ps, pk = nextP()
                mm(ps, blk, Y1[:, c, :], True, True, ["cons", "Y1"], [pk])
                ts(Y1[:, c, :], ps, 1.0 / 64, 64e-5, ALU.mult, ALU.add, [pk], ["Y1"])
            rsqrt_(Y1, "Y1")
            tt(Y0, Y0, Y1, ALU.mult, ["Y0", "Y1"], ["Y0"])
            for c in range(4):
                ts(Y0[:, c, :], Y0[:, c, :], V(("gnw", l), c, 1), V(("gnb", l), c, 1), ALU.mult, ALU.add, ["Y0", "vecs"], ["Y0"])
            S.dma(Y1, fmv("bonus")[:, :, tsl], ["bonus" + g], ["Y1"])
            tt(Y0, Y0, Y1, ALU.add, ["Y0", "Y1"], ["Y0"])
            S.dma(rt, scr[g, "rc"][14 * 128:15 * 128, tsl], ["rc" + g], ["rt"])
            for c in range(4):
                ps, pk = nextP()
                mm(ps, ag2[:, c * 128:(c + 1) * 128], rt, True, True, ["ag2", "rt"], [pk])
                tt(HR[:, c, :], Y0[:, c, :], ps, ALU.mult, ["Y0", pk], ["HR"])
            S.dma(t3, fmv("mrg")[:, 8:16, tsl], ["mrg" + g], ["Y0", "Y1"])
            w, wk = nextW()
            S.dma(w[:, 0:4, :], proj_r[l].rearrange("(kc p) n -> p kc n", p=128)[:, :, 0:512], ["proj_r"], [wk])
            S.dma(w[:, 4:8, :], proj_r[l].rearrange("(kc p) n -> p kc n", p=128)[:, :, 512:1024], ["proj_r"], [wk])
            for oc in range(8):
                ps, pk = nextP()
                wv = w[:, 0:4, :] if oc < 4 else w[:, 4:8, :]
                for kc in range(4):
                    mm(ps, wv[:, kc, (oc % 4) * 128:(oc % 4 + 1) * 128], HR[:, kc, :], kc == 0, kc == 3, [wk, "HR"], [pk])
                tt(t3[:, oc, :], t3[:, oc, :], ps, ALU.mult, ["Y0", "Y1", pk], ["Y0", "Y1"])
            tt(mg, mg, t3, ALU.add, ["mg", "Y0", "Y1"], ["mg"])
            for half in range(2):
                w, wk = nextW()
                S.dma(w, w_out[l].rearrange("(kc p) n -> p kc n", p=128)[:, :, half * 512:(half + 1) * 512], ["w_out"], [wk])
                for o4 in range(4):
                    oc = half * 4 + o4
                    ps, pk = nextP()
                    for kc in range(8):
                        mm(ps, w[:, kc, o4 * 128:(o4 + 1) * 128], mg[:, kc, :], kc == 0, kc == 7, [wk, "mg"], [pk])
                    stt(xt[:, oc, :], ps, modt[l][:, 16 + oc, j:j + 1], xt[:, oc, :], ALU.mult, ALU.add, [pk, "xt", "mod"], ["xt"])
            rmsnorm_tile(xt, "xt", mg, "mg", t2, "HMT", rt, "rt",
                         lambda c: A2t[l][:, c, j:j + 1], lambda c: modt[l][:, 24 + c, j:j + 1])
            w1v = ffn_w1[l].rearrange("(kc p) n -> p kc n", p=128)
            w3v = ffn_w3[l].rearrange("(kc p) n -> p kc n", p=128)
            for sg_ in range(0, 22, 4):
                ncc = min(4, 22 - sg_)
                wa, wak = nextW()
                S.dma(wa[:, :, 0:ncc * 128], w1v[:, :, sg_ * 128:(sg_ + ncc) * 128], ["ffn_w1"], [wak])
                wb, wbk = nextW()
                S.dma(wb[:, :, 0:ncc * 128], w3v[:, :, sg_ * 128:(sg_ + ncc) * 128], ["ffn_w3"], [wbk])
                for cc in range(ncc):
                    hc = sg_ + cc
                    p1, p1k = nextP()
                    for kc in range(8):
                        mm(p1, wa[:, kc, cc * 128:(cc + 1) * 128], mg[:, kc, :], kc == 0, kc == 7, [wak, "mg"], [p1k])
                    p3, p3k = nextP()
                    for kc in range(8):
                        mm(p3, wb[:, kc, cc * 128:(cc + 1) * 128], mg[:, kc, :], kc == 0, kc == 7, [wbk, "mg"], [p3k])
                    act(HID[:, hc, :], p1, AF.Silu, [p1k], ["HID"])
                    tt(HID[:, hc, :], HID[:, hc, :], p3, ALU.mult, ["HID", p3k], ["HID"])
            w2v = ffn_w2[l].rearrange("(kc p) n -> p kc n", p=128)
            for oc in range(8):
                wf = WF[oc % 2]; wfk = "WF%d" % (oc % 2)
                S.dma(wf, w2v[:, :, oc * 128:(oc + 1) * 128], ["ffn_w2"], [wfk])
                ps, pk = nextP()
                for kc in range(22):
                    mm(ps, wf[:, kc, :], HID[:, kc, :], kc == 0, kc == 21, [wfk, "HID"], [pk])
                stt(xt[:, oc, :], ps, modt[l][:, 40 + oc, j:j + 1], xt[:, oc, :], ALU.mult, ALU.add, [pk, "xt", "mod"], ["xt"])
            if is_last:
                rmsnorm_tile(xt, "xt", mg, "mg", t2, "HMT", rt, "rt", lambda c: V("fg", c, 1), None)
                S.dma(yout[g].rearrange("(c p) t -> p c t", p=128)[:, :, tsl], mg, ["mg"], ["yout" + g], eng=STORE_Q)
            else:
                S.dma(scr[g, "xT"].rearrange("(c p) t -> p c t", p=128)[:, :, tsl], xt, ["xt"], ["x_" + g], eng=STORE_Q)

    for l in range(nlayers):
        if "A" in phases:
            phase_A(l)
        for (g, j) in (("P", 0), ("S", 1)):
            if g not in groups:
                continue
            if "B" in phases:
                phase_B(l, g, j)
            if "C" in phases:
                phase_C(l, g)
            if "D" in phases:
                phase_D(l, g)
            if "E" in phases:
                phase_E(l, g)
            if "F" in phases:
                phase_F(l, g)
            if "G" in phases:
                phase_G(l, g, j, l == nlayers - 1)
    S.finish()
    return nc, len(S.ops)


def make_in_maps(inp):
    f = lambda a: np.ascontiguousarray(np.asarray(a, np.float32))
    shared = {k: f(inp[k]) for k in ("ada_w", "w_in", "proj_m", "proj_r", "w_out", "ffn_w1", "ffn_w3", "ffn_w2")}
    shared["r_w2"] = f(inp["r_w2"]).reshape(DEPTH, 128, 512)
    shared["r_a2"] = f(inp["r_a2"]).reshape(DEPTH, 128, 512)
    shared["r_g2"] = f(inp["r_g2"])
    shared["vecs"] = pack_vecs(inp)
    shared["rows"] = pack_rows(inp)
    shared["consts"] = make_consts()
    maps = []
    xp = f(inp["x_prompt"])
    xs = f(inp["x_sample"])
    for c in range(8):
        b = c // 4
        m = dict(shared)
        m["xpT"] = np.ascontiguousarray(xp[4 * c:4 * c + 4].reshape(NP_, DM).T)
        m["xsT"] = np.ascontiguousarray(xs[b].T)
        cv = np.stack([fm(inp["c_ctx"], 8), fm(inp["c"][b], 8)], axis=2)
        m["cvec"] = np.ascontiguousarray(cv.reshape(128, 16))
        C = f(inp["state_mlstm_c"])[b].transpose(0, 1, 3, 2, 4)
        n = f(inp["state_mlstm_n"])[b].transpose(0, 1, 3, 2)[..., None]
        m["cst"] = np.ascontiguousarray(np.concatenate([C, n], axis=-1).reshape(DEPTH, 2, 128, 4 * 129))
        m["mst"] = np.ascontiguousarray(np.broadcast_to(f(inp["state_mlstm_m"])[b][:, :, None, :], (DEPTH, 2, 128, 4)))
        m["rst"] = np.ascontiguousarray(f(inp["state_rwkv"])[b].transpose(0, 1, 4, 2, 3).reshape(DEPTH, 2, 64, 512))
        maps.append(m)
    return maps


_CACHE = {}


def kernel(**inputs):
    if "nc" not in _CACHE:
        _CACHE["nc"] = build()[0]
    nc = _CACHE["nc"]
    maps = make_in_maps(inputs)
    res = run_bass_kernel_spmd(nc, maps, core_ids=list(range(8))).results
    yp = np.concatenate([res[c]["ypT"].T.reshape(4, 256, DM) for c in range(8)], axis=0)
    ys = np.stack([res[0]["ysT"].T, res[4]["ysT"].T], axis=0)
    mc = np.concatenate([res[c]["o_mc"].reshape(4, DEPTH, 2, 128, 4, 129) for c in range(8)], axis=0)
    new_c = np.ascontiguousarray(mc[..., :128].transpose(0, 1, 2, 4, 3, 5))
    new_n = np.ascontiguousarray(mc[..., 128].transpose(0, 1, 2, 4, 3))
    new_m = np.concatenate([res[c]["o_mm"] for c in range(8)], axis=0)
    rs = np.concatenate([res[c]["o_rs"].reshape(4, DEPTH, 2, 64, 8, 64) for c in range(8)], axis=0)
    new_s = np.ascontiguousarray(rs.transpose(0, 1, 2, 4, 5, 3))
    return (yp.astype(np.float32), ys.astype(np.float32), new_c.astype(np.float32), new_n.astype(np.float32),
            new_m.astype(np.float32), new_s.astype(np.float32))
```

```python
from contextlib import ExitStack
import os
import re
FCUT = int(os.environ.get('FCUT', '99'))
import numpy as np
import concourse.bass as bass
import concourse.mybir as mybir
from concourse.bass_utils import run_bass_kernel_spmd

F32 = mybir.dt.float32
ALU = mybir.AluOpType
AF = mybir.ActivationFunctionType
AX = mybir.AxisListType


class _Op:
    __slots__ = ("eng", "fn", "deps", "is_dma", "needs_inc", "sem", "val", "extra_waits")


class Sched:
    ENGS = ("pe", "act", "dve", "pool", "sp")

    def __init__(self, nc, n_dma_sems=24):
        self.nc = nc
        self.ops = []
        self.lastw = {}
        self.readers = {}
        self.last_on = {}
        self.stack = ExitStack()
        self.n_dma_sems = n_dma_sems
        self._nid = 0

    def sb(self, name, shape, dtype=F32):
        return self.stack.enter_context(self.nc.sbuf_tensor(name, list(shape), dtype))

    def ps(self, name, shape, dtype=F32):
        return self.stack.enter_context(self.nc.psum_tensor(name, list(shape), dtype))

    def add(self, eng, fn, reads=(), writes=(), dma=False):
        deps = set()
        for k in reads:
            d = self.lastw.get(k)
            if d is not None:
                deps.add(d)
        for k in writes:
            d = self.lastw.get(k)
            if d is not None:
                deps.add(d)
            deps.update(self.readers.get(k, ()))
        op = _Op()
        op.eng, op.fn, op.deps, op.is_dma = eng, fn, deps, dma
        op.needs_inc, op.sem, op.val, op.extra_waits = False, None, 0, []
        i = len(self.ops)
        self.ops.append(op)
        for k in reads:
            self.readers.setdefault(k, []).append(i)
        for k in writes:
            self.lastw[k] = i
            self.readers[k] = []
        return i

    def pe(self, fn, reads=(), writes=()):
        return self.add("pe", fn, reads, writes)

    def act(self, fn, reads=(), writes=()):
        return self.add("act", fn, reads, writes)

    def dve(self, fn, reads=(), writes=()):
        return self.add("dve", fn, reads, writes)

    def pool(self, fn, reads=(), writes=()):
        return self.add("pool", fn, reads, writes)

    def dma(self, out, in_, reads=(), writes=(), eng="sp", **kw):
        return self.add(eng, lambda e: e.dma_start(out=out, in_=in_, **kw), reads, writes, dma=True)

    def finish(self):
        nc, ops = self.nc, self.ops
        def synced(o, od):
            return od.is_dma or od.eng != o.eng or o.eng != "pe"
        for o in ops:
            for d in o.deps:
                od = ops[d]
                if synced(o, od):
                    od.needs_inc = True
        for o in ops:
            if o.is_dma:
                o.needs_inc = True
        esem = {e: self.stack.enter_context(nc.semaphore("s_" + e)) for e in self.ENGS}
        dsems = [self.stack.enter_context(nc.semaphore("d%d" % i)) for i in range(self.n_dma_sems)]
        cnt = {e: 0 for e in self.ENGS}
        dcnt = [0] * self.n_dma_sems
        dlast = [None] * self.n_dma_sems
        j = 0
        for o in ops:
            if not o.needs_inc:
                continue
            if o.is_dma:
                s = j % self.n_dma_sems
                j += 1
                if dlast[s] is not None:
                    o.extra_waits.append(dlast[s])
                dcnt[s] += 16
                o.sem, o.val = dsems[s], dcnt[s]
                dlast[s] = (dsems[s], dcnt[s])
            else:
                cnt[o.eng] += 1
                o.sem, o.val = esem[o.eng], cnt[o.eng]
        per = {e: [] for e in self.ENGS}
        for o in ops:
            per[o.eng].append(o)
        final_waits = [dl for dl in dlast if dl is not None]
        engobj = {"pe": "tensor", "act": "scalar", "dve": "vector", "pool": "gpsimd", "sp": "sync"}

        def emit(ename, eng):
            waited = {}
            for o in per[ename]:
                ws = list(o.extra_waits)
                for d in o.deps:
                    od = ops[d]
                    if synced(o, od):
                        ws.append((od.sem, od.val))
                best = {}
                for (s, v) in ws:
                    key = id(s)
                    if key not in best or best[key][1] < v:
                        best[key] = (s, v)
                for key, (s, v) in best.items():
                    if waited.get(key, 0) < v:
                        eng.wait_ge(s, v)
                        waited[key] = v
                ins = o.fn(eng)
                if o.needs_inc:
                    ins.then_inc(o.sem, 16 if o.is_dma else 1)
            if ename == "sp":
                for (s, v) in final_waits:
                    if waited.get(id(s), 0) < v:
                        eng.wait_ge(s, v)
                        waited[id(s)] = v

        with nc.Block() as block:
            for ename in self.ENGS:
                getattr(block, engobj[ename])(lambda eng, _n=ename: emit(_n, eng))
        self.stack.close()
        return nc


DEPTH, DM, PIN, FH = 2, 1024, 6032, 2816
NP_, NS_ = 1024, 4096
NT = 512
STORE_Q = "act"
NEG = -1.0e30
WSC = -0.6065306597126334


def _layout(entries):
    off, d = 0, {}
    for name, w in entries:
        d[name] = (off, w)
        off += w
    return d, off


def vec_layout():
    e = [("fg", 8)]
    for l in range(DEPTH):
        e += [(("ada_b", l), 48), (("n1g", l), 8), (("n2g", l), 8), (("mconv", l), 72), (("rconv", l), 135),
              (("a0", l, 0), 4), (("a0", l, 1), 4), (("kk", l), 4), (("ka", l), 4), (("rk", l), 4),
              (("gnw", l), 4), (("gnb", l), 4)]
    return _layout(e)


def row_layout():
    e = []
    for l in range(DEPTH):
        e += [(("gb", l), 16), (("mng", l), 512), (("w0", l, 0), 512), (("w0", l, 1), 512)]
    return _layout(e)


def const_layout():
    e = [("ident", 128), ("ones", 128), ("blk", 128)]
    for d in range(2):
        e += [(("tri", d), 64), (("mneg", d), 256), (("mpos", d), 256), (("tri2", d), 128),
              (("m1mask", d), 512), (("qmask", d), 512), (("akmask", d), 512)]
    return _layout(e)


def make_consts():
    lay, n = const_layout()
    c = np.zeros((128, n), np.float32)

    def put(name, arr):
        o, w = lay[name]
        a = np.asarray(arr, np.float32)
        c[: a.shape[0], o:o + w] = a.reshape(a.shape[0], w)
    put("ident", np.eye(128))
    put("ones", np.ones((128, 128)))
    blk = np.zeros((128, 128))
    blk[:64, :64] = 1
    blk[64:, 64:] = 1
    put("blk", blk)
    s = np.arange(64)[:, None]
    t = np.arange(64)[None, :]
    for d in range(2):
        inc = (s <= t) if d == 0 else (s >= t)
        strict = (s < t) if d == 0 else (s > t)
        put(("tri", d), inc.astype(np.float32))
        put(("mneg", d), np.tile(np.where(inc.T, 0.0, NEG)[:, None, :], (1, 4, 1)))
        put(("mpos", d), np.tile(np.where(inc, 0.0, -NEG)[:, None, :], (1, 4, 1)))
        put(("tri2", d), np.concatenate([inc, strict], 1).astype(np.float32) * WSC)
        m1 = np.concatenate([strict, inc], 1).astype(np.float32)
        m1 = np.concatenate([m1, m1], 0)
        put(("m1mask", d), np.tile(m1[:, None, :], (1, 4, 1)))
        put(("qmask", d), np.tile(strict.T.astype(np.float32)[:, None, :], (1, 8, 1)))
        put(("akmask", d), np.tile(strict.astype(np.float32)[:, None, :], (1, 8, 1)))
    return c


def fm(v, nchunk):
    return np.ascontiguousarray(np.asarray(v, np.float32).reshape(nchunk, 128).T)


def pack_vecs(inp):
    lay, n = vec_layout()
    c = np.zeros((128, n), np.float32)

    def put(name, arr):
        o, w = lay[name]
        c[:, o:o + w] = arr.reshape(128, w)
    put("fg", fm(inp["final_g"], 8))
    for l in range(DEPTH):
        put(("ada_b", l), fm(inp["ada_b"][l], 48))
        put(("n1g", l), fm(inp["norm1_g"][l], 8))
        put(("n2g", l), fm(inp["norm2_g"][l], 8))
        put(("mconv", l), np.ascontiguousarray(inp["m_conv"][l].reshape(9, 8, 128).transpose(2, 1, 0)))
        put(("rconv", l), np.ascontiguousarray(inp["r_conv"][l].reshape(9, 15, 128).transpose(2, 1, 0)))
        for d in range(2):
            put(("a0", l, d), fm(inp["r_a0"][l, d], 4))
        put(("kk", l), fm(inp["r_kk"][l], 4))
        put(("ka", l), fm(inp["r_ka"][l], 4))
        put(("rk", l), fm(inp["r_rk"][l], 4))
        put(("gnw", l), fm(inp["r_gn_w"][l], 4))
        put(("gnb", l), fm(inp["r_gn_b"][l], 4))
    return c


def pack_rows(inp):
    lay, n = row_layout()
    c = np.zeros((128, n), np.float32)

    def put(name, arr):
        o, w = lay[name]
        c[:, o:o + w] = np.asarray(arr, np.float32).reshape(1, w)
    for l in range(DEPTH):
        put(("gb", l), inp["m_gate_b"][l])
        put(("mng", l), inp["m_norm_g"][l])
        for d in range(2):
            put(("w0", l, d), inp["r_w0"][l, d])
    return c


def build(debug=False, nlayers=DEPTH, phases="ABCDEFG", groups="PS", nseq_p=4):
    nc = bass.Bass("TRN2", target_bir_lowering=False)
    S = Sched(nc)
    ARENA = 53200
    arena = S.sb("arena", [128, ARENA])
    PS = S.ps("ps", [128, 4096])
    aoff = [0]

    def A(shape, p0=0):
        n = int(np.prod(shape[1:]))
        o = aoff[0]
        aoff[0] += n
        assert aoff[0] <= ARENA, aoff[0]
        v = arena[p0:p0 + shape[0], o:o + n]
        if len(shape) == 3:
            v = v.rearrange("p (a b) -> p a b", b=shape[2])
        elif len(shape) == 4:
            v = v.rearrange("p (a b c) -> p a b c", b=shape[2], c=shape[3])
        return v

    def din(name, shape):
        return nc.dram_tensor(name, list(shape), F32, kind="ExternalInput").ap()

    def dout(name, shape):
        return nc.dram_tensor(name, list(shape), F32, kind="ExternalOutput").ap()

    def dscr(name, shape):
        return nc.dram_tensor(name, list(shape), F32, kind="ExternalOutput" if debug else "Internal").ap()

    def bank(b, parts=128, cols=512, c0=0):
        return PS[0:parts, b * 512 + c0:b * 512 + c0 + cols]

    def mm(out, lhsT, rhs, start, stop, reads, writes):
        S.pe(lambda e: e.matmul(out, lhsT, rhs, start=start, stop=stop), reads, writes)

    def act(out, in_, func, reads, writes, bias=None, scale=1.0):
        if bias is None:
            S.act(lambda e: e.activation(out=out, in_=in_, func=func, scale=scale), reads, writes)
        else:
            S.act(lambda e: e.activation(out=out, in_=in_, func=func, bias=bias, scale=scale), reads, writes)

    def tt(out, in0, in1, op, reads, writes, eng="dve"):
        S.add(eng, lambda e: e.tensor_tensor(out, in0, in1, op), reads, writes)

    def ts(out, in0, s1, s2, op0, op1, reads, writes, eng="dve"):
        if s2 is None:
            S.add(eng, lambda e: e.tensor_scalar(out, in0, s1, None, op0), reads, writes)
        else:
            S.add(eng, lambda e: e.tensor_scalar(out, in0, s1, s2, op0, op1), reads, writes)

    def stt(out, in0, sc, in1, op0, op1, reads, writes, eng="dve"):
        S.add(eng, lambda e: e.scalar_tensor_tensor(out, in0, sc, in1, op0, op1), reads, writes)

    def cp(out, in_, reads, writes):
        S.dve(lambda e: e.tensor_copy(out, in_), reads, writes)

    def red(out, in_, op, reads, writes):
        S.dve(lambda e: e.tensor_reduce(out, in_, AX.X, op), reads, writes)

    def rsqrt_(ap, key):
        act(ap, ap, AF.Ln, [key], [key])
        act(ap, ap, AF.Exp, [key], [key], scale=-0.5)

    def bc(ap, axis, shape):
        return ap.unsqueeze(axis).to_broadcast(list(shape))

    xin = {"P": din("xpT", [DM, NP_]), "S": din("xsT", [DM, NS_])}
    cvec_d = din("cvec", [128, 16])
    ada_w = din("ada_w", [DEPTH, DM, 6 * DM])
    w_in = din("w_in", [DEPTH, DM, PIN])
    proj_m = din("proj_m", [DEPTH, 512, DM])
    proj_r = din("proj_r", [DEPTH, 512, DM])
    w_out = din("w_out", [DEPTH, DM, DM])
    ffn_w1 = din("ffn_w1", [DEPTH, DM, FH])
    ffn_w3 = din("ffn_w3", [DEPTH, DM, FH])
    ffn_w2 = din("ffn_w2", [DEPTH, FH, DM])
    r_w2 = din("r_w2", [DEPTH, 128, 512])
    r_a2 = din("r_a2", [DEPTH, 128, 512])
    r_g2 = din("r_g2", [DEPTH, 128, 512])
    vlay, nvec = vec_layout()
    rlay, nrow = row_layout()
    clay, ncon = const_layout()
    vecs_d = din("vecs", [128, nvec])
    rows_d = din("rows", [128, nrow])
    cons_d = din("consts", [128, ncon])
    cst_d = din("cst", [DEPTH, 2, 128, 4 * 129])
    mst_d = din("mst", [DEPTH, 2, 128, 4])
    rst_d = din("rst", [DEPTH, 2, 64, 512])

    yout = {"P": dout("ypT", [DM, NP_]), "S": dout("ysT", [DM, NS_])}
    o_mc = dout("o_mc", [4, DEPTH, 2, 128, 4 * 129])
    o_mm = dout("o_mm", [4, DEPTH, 2, 4])
    o_rs = dout("o_rs", [4, DEPTH, 2, 64, 512])

    GN = {"P": NP_, "S": NS_}
    scr = {}
    for g in "PS":
        N = GN[g]
        for nm, shp in [("xT", [DM, N]), ("uqk", [1024, N]), ("ur", [1920, N]), ("mrg", [2048, N]),
                        ("uvo", [N, 1040]), ("qk", [1024, N]), ("rc", [1920, N]), ("av", [512, N]),
                        ("bonus", [512, N]), ("kt0", [512, N]), ("kt1", [512, N]), ("bv0", [512, N]),
                        ("bv1", [512, N]), ("sg0", [N, 512]), ("sg1", [N, 512]), ("vrt", [N, 512]),
                        ("hm0", [N, 512]), ("hm1", [N, 512]), ("y0", [512, N]), ("y1", [512, N])]:
            scr[g, nm] = dscr("%s_%s" % (nm, g), shp)

    cons = A([128, ncon])
    vecs = A([128, nvec])
    rows = A([128, nrow])
    cv = A([128, 16])
    modt = [A([128, 48, 2]) for _ in range(DEPTH)]
    A1t = [A([128, 8, 2]) for _ in range(DEPTH)]
    A2t = [A([128, 8, 2]) for _ in range(DEPTH)]
    omka = [A([128, 4]) for _ in range(DEPTH)]
    aw2 = A([128, 512])
    aa2 = A([128, 512])
    ag2 = A([128, 512])
    persist_end = aoff[0]

    def C(name, parts=128, c0=0, cols=None):
        o, w = clay[name]
        return cons[0:parts, o + c0:o + c0 + (w if cols is None else cols)]

    def V(name, c0=0, cols=None):
        o, w = vlay[name]
        return vecs[:, o + c0:o + c0 + (w if cols is None else cols)]

    def R(name, parts=128):
        o, w = rlay[name]
        return rows[0:parts, o:o + w]

    S.dma(cons, cons_d, ["cons_d"], ["cons"])
    S.dma(vecs, vecs_d, ["vecs_d"], ["vecs"])
    S.dma(rows, rows_d, ["rows_d"], ["rows"])
    S.dma(cv, cvec_d, ["cvec_d"], ["cv"])
    act(cv, cv, AF.Silu, ["cv"], ["cv"])
    ident = C("ident")
    ones = C("ones")
    blk = C("blk")
    id64 = C("ident", 64, 0, 64)

    bstate = {"deps": []}

    def barrier():
        last = {}
        alld = []
        for i, o in enumerate(S.ops):
            if o.is_dma:
                alld.append(i)
            else:
                last[o.eng] = i
        S.lastw["__bar__"] = None
        ids = list(last.values()) + alld[-(3 * S.n_dma_sems):]
        bstate["deps"] = ids
        aoff[0] = persist_end

    _add = S.add

    cur = {"d": 0, "vmap": None}
    _psre = re.compile(r"^ps(\d)")

    def kx(key):
        if cur["vmap"] is not None and isinstance(key, str):
            m = _psre.match(key)
            if m:
                return "ps%d" % (4 * cur["d"] + cur["vmap"][int(m.group(1))])
        return key

    def bank_(v, parts=128, cols=512, c0=0):
        return bank(4 * cur["d"] + cur["vmap"][v], parts, cols, c0)

    def drive(chains):
        active = list(chains)
        while active:
            for item in list(active):
                cur["d"] = item[0]
                try:
                    next(item[1])
                except StopIteration:
                    active.remove(item)

    def add_with_barrier(eng, fn, reads=(), writes=(), dma=False):
        reads = [kx(k_) for k_ in reads]
        writes = [kx(k_) for k_ in writes]
        i = _add(eng, fn, reads, writes, dma)
        S.ops[i].deps.update(bstate["deps"])
        return i
    S.add = add_with_barrier

    def phase_A(l):
        barrier()
        W = [A([128, 8, 512]), A([128, 8, 512])]
        aw = ada_w[l].rearrange("(kc p) n -> p kc n", p=128)
        mps = bank(0, 128, 96)
        for ng in range(12):
            w = W[ng % 2]
            wk = "W%d" % (ng % 2)
            S.dma(w, aw[:, :, ng * 512:(ng + 1) * 512], ["ada_w"], [wk])
            for nn in range(4):
                ch = ng * 4 + nn
                for kc in range(8):
                    mm(mps[:, ch * 2:ch * 2 + 2], w[:, kc, nn * 128:(nn + 1) * 128], cv[:, kc * 2:kc * 2 + 2],
                       kc == 0, kc == 7, [wk, "cv"], ["ps0"])
        mk = "mod%d" % l
        tt(modt[l], mps.rearrange("p (c j) -> p c j", j=2), bc(V(("ada_b", l)), 2, [128, 48, 2]), ALU.add,
           ["ps0", "vecs"], [mk])
        for (At, c0, gname) in ((A1t[l], 8, "n1g"), (A2t[l], 32, "n2g")):
            ts(At, modt[l][:, c0:c0 + 8, :], 1.0, None, ALU.add, None, [mk], [mk + "A"])
            tt(At, At, bc(V((gname, l)), 2, [128, 8, 2]), ALU.mult, [mk + "A", "vecs"], [mk + "A"])
        ts(omka[l], V(("ka", l)), -1.0, 1.0, ALU.mult, ALU.add, ["vecs"], ["omka"])
        if debug and l == 0:
            dm = dout("dbg_mod", [128, 96 + 16 + 16 + 16])
            S.dma(dm[:, 0:96], modt[l].rearrange("p c j -> p (c j)"), [mk], ["dbg"])
            S.dma(dm[:, 96:112], A1t[l].rearrange("p c j -> p (c j)"), [mk + "A"], ["dbg"])
            S.dma(dm[:, 112:128], A2t[l].rearrange("p c j -> p (c j)"), [mk + "A"], ["dbg"])
            S.dma(dm[:, 128:144], cv, ["cv"], ["dbg"])

    def rmsnorm_tile(xt, xk, ht, hk, sq, sqk, rt, rk_, Aap, Bap):
        act(sq, xt, AF.Square, [xk], [sqk])
        ps = bank(7)
        for c in range(8):
            mm(ps, ones, sq[:, c, :], c == 0, c == 7, ["cons", sqk], ["ps7"])
        ts(rt, ps, 1.0 / DM, 1e-6, ALU.mult, ALU.add, ["ps7"], [rk_])
        rsqrt_(rt, rk_)
        tt(ht, xt, bc(rt, 1, [128, 8, 512]), ALU.mult, [xk, rk_], [hk])
        for c in range(8):
            if Bap is None:
                ts(ht[:, c, :], ht[:, c, :], Aap(c), None, ALU.mult, None, [hk, "vecs", "mod"], [hk])
            else:
                ts(ht[:, c, :], ht[:, c, :], Aap(c), Bap(c), ALU.mult, ALU.add, [hk, "vecs", "mod"], [hk])

    def phase_B(l, g, j):
        barrier()
        N = GN[g]
        xsrc = (xin[g] if l == 0 else scr[g, "xT"]).rearrange("(c p) t -> p c t", p=128)
        xt = A([128, 8, 512]); ht = A([128, 8, 512]); sq = A([128, 8, 512]); rt = A([128, 512])
        W = [A([128, 8, 512]), A([128, 8, 512])]
        OUT = [A([128, 4, 512]), A([128, 4, 512])]
        win = w_in[l].rearrange("(kc p) n -> p kc n", p=128)
        wcnt = [0]
        ocnt = [0]
        pcnt = [0]
        for ti in range(N // NT):
            tsl = slice(ti * NT, (ti + 1) * NT)
            S.dma(xt, xsrc[:, :, tsl], ["x_" + g], ["xt"])
            rmsnorm_tile(xt, "xt", ht, "ht", sq, "sq", rt, "rt",
                         lambda c: A1t[l][:, c, j:j + 1], lambda c: modt[l][:, c, j:j + 1])
            for (nm, col0, nch, fn) in (("uqk", 0, 8, AF.Copy), ("ur", 2064, 15, AF.Copy), ("mrg", 3984, 16, AF.Sigmoid)):
                dst = scr[g, nm].rearrange("(c p) t -> p c t", p=128)
                for cg in range(0, nch, 4):
                    ncc = min(4, nch - cg)
                    wi_ = wcnt[0] % 2; wcnt[0] += 1
                    w = W[wi_]; wk = "W%d" % wi_
                    S.dma(w[:, :, 0:ncc * 128], win[:, :, col0 + cg * 128:col0 + (cg + ncc) * 128], ["w_in"], [wk])
                    oi = ocnt[0] % 2; ocnt[0] += 1
                    o = OUT[oi]; ok = "OUT%d" % oi
                    for cc in range(ncc):
                        pb = pcnt[0] % 4; pcnt[0] += 1
                        ps = bank(pb)
                        for kc in range(8):
                            mm(ps, w[:, kc, cc * 128:(cc + 1) * 128], ht[:, kc, :], kc == 0, kc == 7,
                               [wk, "ht"], ["ps%d" % pb])
                        act(o[:, cc, :], ps, fn, ["ps%d" % pb], [ok])
                    S.dma(dst[:, cg:cg + ncc, tsl], o[:, 0:ncc, :], [ok], [nm + g], eng=STORE_Q)
            for (c0, ncol) in ((1024, 512), (1536, 512), (2048, 16)):
                wi_ = wcnt[0] % 2; wcnt[0] += 1
                w = W[wi_]; wk = "W%d" % wi_
                S.dma(w[:, :, 0:ncol], win[:, :, c0:c0 + ncol], ["w_in"], [wk])
                oi = ocnt[0] % 2; ocnt[0] += 1
                o = OUT[oi]; ok = "OUT%d" % oi
                for t4 in range(4):
                    pb = pcnt[0] % 4; pcnt[0] += 1
                    ps = bank(pb, 128, ncol)
                    for kc in range(8):
                        mm(ps, ht[:, kc, t4 * 128:(t4 + 1) * 128], w[:, kc, 0:ncol], kc == 0, kc == 7,
                           [wk, "ht"], ["ps%d" % pb])
                    act(o[:, t4, 0:ncol], ps, AF.Copy, ["ps%d" % pb], [ok])
                dstv = scr[g, "uvo"][tsl, c0 - 1024:c0 - 1024 + ncol].rearrange("(t p) n -> p t n", p=128)
                S.dma(dstv, o[:, :, 0:ncol], [ok], ["uvo" + g], eng=STORE_Q)

    def phase_C(l, g):
        barrier()
        N = GN[g]
        bufs = [(A([128, 4096]), A([128, 4096])), (A([128, 4096]), A([128, 4096]))]
        if g == "S":
            Rr, Wd = 64, 64
            taps = [(dy, dx) for dy in range(3) for dx in range(3)]
        else:
            Rr, Wd = 4, 256
            taps = [(1, 0), (1, 1), (1, 2)]
        k = 0
        for (src, dstn, nch, vname) in (("uqk", "qk", 8, "mconv"), ("ur", "rc", 15, "rconv")):
            for c in range(nch):
                cin_, acc_ = bufs[k % 2]
                ik, ak = "cin%d" % (k % 2), "acc%d" % (k % 2)
                k += 1
                cin = cin_[:, 0:N]; acc = acc_[:, 0:N]
                S.dma(cin, scr[g, src][c * 128:(c + 1) * 128, :], [src + g], [ik])
                c3 = cin.rearrange("p (r w) -> p r w", w=Wd)
                a3 = acc.rearrange("p (r w) -> p r w", w=Wd)
                wv = lambda dy, dx: V((vname, l), c * 9 + dy * 3 + dx, 1)
                ts(acc, cin, wv(1, 1), None, ALU.mult, None, [ik, "vecs"], [ak])
                ti_ = 0
                for (dy, dx) in taps:
                    if (dy, dx) == (1, 1):
                        continue
                    oy, ox = dy - 1, dx - 1
                    r0, r1 = max(0, -oy), Rr - max(0, oy)
                    q0, q1 = max(0, -ox), Wd - max(0, ox)
                    eng = "dve"
                    ti_ += 1
                    stt(a3[:, r0:r1, q0:q1], c3[:, r0 + oy:r1 + oy, q0 + ox:q1 + ox], wv(dy, dx), a3[:, r0:r1, q0:q1],
                        ALU.mult, ALU.add, [ik, ak, "vecs"], [ak], eng=eng)
                if src == "uqk":
                    act(acc, acc, AF.Silu, [ak], [ak])
                    if c < 4:
                        ts(acc, acc, 128.0 ** -0.5, None, ALU.mult, None, [ak], [ak])
                elif c == 12:
                    act(acc, acc, AF.Tanh, [ak], [ak])
                elif c == 14:
                    act(acc, acc, AF.Sigmoid, [ak], [ak])
                S.dma(scr[g, dstn][c * 128:(c + 1) * 128, :], acc, [ak], [dstn + g], eng=STORE_Q)

    def phase_D(l, g):
        barrier()
        N = GN[g]
        S.dma(aw2, r_w2[l], ["r_w2"], ["aw2"])
        S.dma(aa2, r_a2[l], ["r_a2"], ["aa2"])
        RK = A([128, 8, 512]); VV = A([128, 6, 512]); T2 = A([128, 8, 512]); T3 = A([128, 8, 512])
        STG = A([128, 4, 512])
        rcv = scr[g, "rc"].rearrange("(c p) t -> p c t", p=128)
        fmv = lambda nm: scr[g, nm].rearrange("(c p) t -> p c t", p=128)
        tmv = lambda nm, tsl: scr[g, nm][tsl, :].rearrange("(t p) n -> p t n", p=128)
        for ti in range(N // NT):
            tsl = slice(ti * NT, (ti + 1) * NT)
            S.dma(RK, rcv[:, 0:8, tsl], ["rc" + g], ["RK"])
            S.dma(VV, rcv[:, 8:14, tsl], ["rc" + g], ["VV"])
            for c in range(4):
                ts(T2[:, c, :], RK[:, 4 + c, :], V(("kk", l), c, 1), None, ALU.mult, None, ["RK", "vecs"], ["T2a"])
            tt(T2[:, 4:8, :], T2[:, 0:4, :], T2[:, 0:4, :], ALU.mult, ["T2a"], ["T2b"])
            for c in range(4):
                mm(bank(c), blk, T2[:, 4 + c, :], True, True, ["cons", "T2b"], ["ps%d" % c])
                ts(T3[:, c, :], bank(c), 1e-24, None, ALU.max, None, ["ps%d" % c], ["T3a"])
            rsqrt_(T3[:, 0:4, :], "T3a")
            stt(T3[:, 4:8, :], T2[:, 0:4, :], -1.0, T3[:, 0:4, :], ALU.mult, ALU.mult, ["T2a", "T3a"], ["T3b"])
            S.dma(fmv("av")[:, :, tsl], T3[:, 4:8, :], ["T3b"], ["av" + g], eng=STORE_Q)
            for c in range(4):
                stt(T2[:, 4 + c, :], RK[:, c, :], V(("rk", l), c, 1), RK[:, 4 + c, :], ALU.mult, ALU.mult,
                    ["RK", "vecs"], ["T2b"])
                mm(bank(4 + c), blk, T2[:, 4 + c, :], True, True, ["cons", "T2b"], ["ps%d" % (4 + c)])
                tt(T3[:, c, :], bank(4 + c), VV[:, c, :], ALU.mult, ["ps%d" % (4 + c), "VV"], ["T3a"])
            S.dma(fmv("bonus")[:, :, tsl], T3[:, 0:4, :], ["T3a"], ["bonus" + g], eng=STORE_Q)
            for d in range(2):
                pb_ = 64 * d
                for c in range(4):
                    mm(bank(c), aa2[pb_:pb_ + 64, c * 128:(c + 1) * 128], VV[pb_:pb_ + 64, 5, :], True, True,
                       ["aa2", "VV"], ["ps%d" % c])
                    act(T2[:, c, :], bank(c), AF.Sigmoid, ["ps%d" % c, "vecs"], ["T2a"], bias=V(("a0", l, d), c, 1))
                    ts(T2[:, 4 + c, :], T2[:, c, :], V(("ka", l), c, 1), omka[l][:, c:c + 1], ALU.mult, ALU.add,
                       ["T2a", "vecs", "omka"], ["T2b"])
                tt(T2[:, 4:8, :], T2[:, 4:8, :], RK[:, 4:8, :], ALU.mult, ["T2b", "RK"], ["T2b"])
                S.dma(fmv("kt%d" % d)[:, :, tsl], T2[:, 4:8, :], ["T2b"], ["kt%d" % d + g], eng=STORE_Q)
                stt(T3[:, 0:4, :], T3[:, 4:8, :], -1.0, T2[:, 0:4, :], ALU.mult, ALU.mult, ["T3b", "T2a"], ["T3a"])
                S.dma(fmv("bv%d" % d)[:, :, tsl], T3[:, 0:4, :], ["T3a"], ["bv%d" % d + g], eng=STORE_Q)
                for t4 in range(4):
                    ps = bank(4 + t4)
                    mm(ps, VV[pb_:pb_ + 64, 4, t4 * 128:(t4 + 1) * 128], aw2[pb_:pb_ + 64, :], True, False,
                       ["VV", "aw2"], ["ps%d" % (4 + t4)])
                    mm(ps, ones[0:1, :], R(("w0", l, d), 1), False, True, ["cons", "rows"], ["ps%d" % (4 + t4)])
                    act(STG[:, t4, :], ps, AF.Sigmoid, ["ps%d" % (4 + t4)], ["STG"])
                S.dma(tmv("sg%d" % d, tsl), STG, ["STG"], ["sg%d" % d + g], eng=STORE_Q)
            for t4 in range(4):
                ps = bank(t4)
                for c in range(4):
                    mm(ps[:, c * 128:(c + 1) * 128], VV[:, c, t4 * 128:(t4 + 1) * 128], ident, True, True,
                       ["VV", "cons"], ["ps%d" % t4])
                act(STG[:, t4, :], ps, AF.Copy, ["ps%d" % t4], ["STG"])
            S.dma(tmv("vrt", tsl), STG, ["STG"], ["vrt" + g], eng=STORE_Q)

    def phase_E(l, g):
        barrier()
        N = GN[g]
        nseq, T = (nseq_p, 256) if g == "P" else (1, 4096)
        nch = T // 64
        qkv = scr[g, "qk"].rearrange("(c p) t -> p c t", p=128)
        uvo = scr[g, "uvo"]
        D = []
        for d in range(2):
            t = dict(QK=A([128, 8, 64]), VT=A([64, 4, 129]), GT=A([64, 16]), CS=A([128, 4, 129]), MR=A([128, 4]),
                     e1=A([64, 4]), nl=A([64, 4]), nbs=A([64, 4]), gg=A([64, 4]), dg=A([64, 4, 64]), cm=A([64, 4]),
                     gmx=A([128, 4]), aa=A([64, 4]), wi=A([64, 4]), emt=A([64, 4]), da=A([64, 4, 64]),
                     wt=A([64, 4, 64]), swt=A([64, 4, 64]), r1=A([64, 4, 129]), dn=A([64, 4]), hout=A([64, 4, 128]),
                     mm_=A([128, 4]), wie=A([128, 4]), we=A([64, 4]), kws=A([64, 4, 128]), nbl=A([128, 4]))
            D.append(t)
            S.add("dve", lambda e, ap=t["VT"]: e.memset(ap, 1.0), [], ["VT%d" % d])

        def chunk(d, seq, ci, first, last):
            t = D[d]
            k = lambda nm: nm + str(d)
            tok = seq * T + ci * 64
            if first:
                if g == "P":
                    S.add("dve", lambda e: e.memset(t["CS"], 0.0), [], [k("CS")])
                    S.add("dve", lambda e: e.memset(t["MR"], 0.0), [], [k("MR")])
                else:
                    S.dma(t["CS"], cst_d[l, d].rearrange("p (h e) -> p h e", e=129), ["cst"], [k("CS")])
                    S.dma(t["MR"], mst_d[l, d], ["mst"], [k("MR")])
            S.dma(t["QK"], qkv[:, :, tok:tok + 64], ["qk" + g], [k("QK")])
            S.dma(t["VT"][:, :, 0:128], uvo[tok:tok + 64, 0:512].rearrange("p (h e) -> p h e", e=128),
                  ["uvo" + g], [k("VT")])
            S.dma(t["GT"], uvo[tok:tok + 64, 1024:1040], ["uvo" + g], [k("GT")])
            tt(t["GT"], t["GT"], R(("gb", l), 64), ALU.add, [k("GT"), "rows"], [k("GT")])
            ic, fc = 8 * d, 8 * d + 4
            act(t["e1"], t["GT"][:, fc:fc + 4], AF.Exp, [k("GT")], [k("e1")], scale=-1.0)
            ts(t["e1"], t["e1"], 1.0, None, ALU.add, None, [k("e1")], [k("e1")])
            act(t["nl"], t["e1"], AF.Ln, [k("e1")], [k("nl")])
            yield
            mm(bank_(0, 64, 4), C(("tri", d), 64), t["nl"], True, True, ["cons", k("nl")], ["ps0a"])
            yield
            mm(bank_(2, 128, 4, 384), ones[0:64, :], t["nl"], True, True, ["cons", k("nl")], ["ps2b"])
            tt(t["gg"], t["GT"][:, ic:ic + 4], bank_(0, 64, 4), ALU.add, [k("GT"), "ps0a", "ps2b"], [k("gg")])
            cp(t["nbs"], bank_(0, 64, 4), ["ps0a", "ps2b"], [k("nbs")])
            cp(t["nbl"], bank_(2, 128, 4, 384), ["ps2b", "ps0a"], [k("nbl")])
            tt(t["dg"], bc(id64, 1, [64, 4, 64]), bc(t["gg"], 2, [64, 4, 64]), ALU.mult, ["cons", k("gg")], [k("dg")])
            dgf = t["dg"].rearrange("p h s -> p (h s)")
            psG = bank_(0, 64, 256, 256)
            yield
            mm(psG, ones[0:64, 0:64], dgf, True, True, ["cons", k("dg")], ["ps0c"])
            tt(t["swt"], psG.rearrange("p (h s) -> p h s", s=64), C(("mneg", d), 64).rearrange("p (h s) -> p h s", s=64),
               ALU.add, ["ps0c", "cons"], [k("swt")])
            red(t["cm"], t["swt"], ALU.max, [k("swt")], [k("cm")])
            psM = bank_(1, 128, 256)
            yield
            mm(psM, ones[0:64, :], dgf, True, True, ["cons", k("dg")], ["ps1a"])
            red(t["gmx"], psM.rearrange("p (h s) -> p h s", s=64), ALU.max, ["ps1a"], [k("gmx")])
            tt(t["aa"], t["cm"], t["MR"][0:64, :], ALU.max, [k("cm"), k("MR")], [k("aa")])
            tt(t["wi"], t["MR"][0:64, :], t["aa"], ALU.subtract, [k("MR"), k("aa")], [k("wi")])
            act(t["wi"], t["wi"], AF.Exp, [k("wi")], [k("wi")])
            tt(t["emt"], t["nbs"], t["aa"], ALU.subtract, [k("nbs"), k("aa")], [k("emt")])
            act(t["emt"], t["emt"], AF.Exp, [k("emt")], [k("emt")])
            tt(t["da"], bc(id64, 1, [64, 4, 64]), bc(t["aa"], 2, [64, 4, 64]), ALU.mult, ["cons", k("aa")], [k("da")])
            psA = bank_(1, 64, 256, 256)
            yield
            mm(psA, ones[0:64, 0:64], t["da"].rearrange("p h s -> p (h s)"), True, True, ["cons", k("da")], ["ps1b"])
            tt(t["wt"], psA.rearrange("p (h s) -> p h s", s=64), bc(t["gg"], 2, [64, 4, 64]), ALU.subtract,
               ["ps1b", k("gg")], [k("wt")])
            tt(t["wt"], t["wt"], C(("mpos", d), 64).rearrange("p (h s) -> p h s", s=64), ALU.add, [k("wt"), "cons"], [k("wt")])
            act(t["wt"], t["wt"], AF.Exp, [k("wt")], [k("wt")], scale=-1.0)
            psS = bank_(2, 64, 256)
            yield
            for h in range(4):
                mm(psS[:, h * 64:(h + 1) * 64], t["QK"][:, 4 + h, :], t["QK"][:, h, :], True, True, [k("QK")], ["ps2a"])
            tt(t["swt"], psS.rearrange("p (h s) -> p h s", s=64), t["wt"], ALU.mult, ["ps2a", k("wt")], [k("swt")])
            yield
            for h in range(4):
                bI = 4 + h // 2
                mm(bank_(bI, 64, 129, (h % 2) * 256), t["swt"][:, h, :], t["VT"][:, h, :], True, True,
                   [k("swt"), k("VT")], ["ps%d" % bI])
                bN = 6 + h // 2
                mm(bank_(bN, 64, 129, (h % 2) * 256), t["QK"][:, h, :], t["CS"][:, h, :], True, True,
                   [k("QK"), k("CS")], ["ps%d" % bN])
            for hb in range(2):
                vN = bank_(6 + hb, 64).rearrange("p (h e) -> p h e", e=256)[:, :, 0:129]
                vI = bank_(4 + hb, 64).rearrange("p (h e) -> p h e", e=256)[:, :, 0:129]
                r1v = t["r1"][:, 2 * hb:2 * hb + 2, :]
                tt(r1v, vN, bc(t["wi"][:, 2 * hb:2 * hb + 2], 2, [64, 2, 129]), ALU.mult,
                   ["ps%d" % (6 + hb), k("wi")], [k("r1")])
                tt(r1v, r1v, vI, ALU.add, [k("r1"), "ps%d" % (4 + hb)], [k("r1")])
            ts(t["dn"], t["r1"][:, :, 128], -1.0, None, ALU.mult, None, [k("r1")], [k("dn")])
            tt(t["dn"], t["dn"], t["r1"][:, :, 128], ALU.max, [k("dn"), k("r1")], [k("dn")])
            tt(t["dn"], t["dn"], t["emt"], ALU.max, [k("dn"), k("emt")], [k("dn")])
            S.dve(lambda e: e.reciprocal(t["dn"], t["dn"]), [k("dn")], [k("dn")])
            tt(t["hout"], t["r1"][:, :, 0:128], bc(t["dn"], 2, [64, 4, 128]), ALU.mult, [k("r1"), k("dn")], [k("hout")])
            S.dma(scr[g, "hm%d" % d][tok:tok + 64, :], t["hout"].rearrange("p h e -> p (h e)"), [k("hout")],
                  ["hm%d" % d + g], eng=STORE_Q)
            if debug and g == "P" and d == 0 and seq == 1 and ci == 0 and l == 0:
                dd = dout("dbg_e", [64, 64])
                for i_, nm_ in enumerate(("nbs", "gg", "cm", "aa", "emt", "wi", "dn", "nl", "e1")):
                    S.dma(dd[:, 4 * i_:4 * i_ + 4], t[nm_], [k(nm_)], ["dbg_e"])
                S.dma(dd[:, 40:56], t["GT"], [k("GT")], ["dbg_e"])
                S.dma(dd[:, 56:60], t["r1"][:, :, 128], [k("r1")], ["dbg_e"], allow_slow_non_contiguous=True)
            tt(t["mm_"], t["MR"], t["gmx"], ALU.max, [k("MR"), k("gmx")], [k("mm_")])
            tt(t["wie"], t["MR"], t["mm_"], ALU.subtract, [k("MR"), k("mm_")], [k("wie")])
            act(t["wie"], t["wie"], AF.Exp, [k("wie")], [k("wie")])
            tt(t["we"], t["gg"], t["mm_"][0:64, :], ALU.subtract, [k("gg"), k("mm_")], [k("we")])
            act(t["we"], t["we"], AF.Exp, [k("we")], [k("we")])
            tt(t["MR"], t["mm_"], t["nbl"], ALU.subtract, [k("mm_"), k("nbl"), k("wi"), k("wie")], [k("MR")])
            psK = bank_(3, 64, 512)
            yield
            for h in range(4):
                mm(psK[:, h * 128:(h + 1) * 128], t["QK"][:, 4 + h, :], ident, True, True, [k("QK"), "cons"], ["ps3"])
            tt(t["kws"], psK.rearrange("p (h e) -> p h e", e=128), bc(t["we"], 2, [64, 4, 128]), ALU.mult,
               ["ps3", k("we")], [k("kws")])
            yield
            for h in range(4):
                bI = 4 + h // 2
                mm(bank_(bI, 128, 129, (h % 2) * 256), t["kws"][:, h, :], t["VT"][:, h, :], True, True,
                   [k("kws"), k("VT")], ["ps%d" % bI])
            tt(t["CS"], t["CS"], bc(t["wie"], 2, [128, 4, 129]), ALU.mult, [k("CS"), k("wie")], [k("CS")])
            for hb in range(2):
                vD = bank_(4 + hb).rearrange("p (h e) -> p h e", e=256)[:, :, 0:129]
                csv = t["CS"][:, 2 * hb:2 * hb + 2, :]
                tt(csv, csv, vD, ALU.add, [k("CS"), "ps%d" % (4 + hb)], [k("CS")])
            if last and g == "P":
                S.dma(o_mc[seq, l, d], t["CS"].rearrange("p h e -> p (h e)"), [k("CS")], ["o_mc"], eng=STORE_Q)
                S.dma(o_mm[seq, l, d:d + 1, :], t["MR"][0:1, :], [k("MR")], ["o_mm"], eng=STORE_Q)

        def chain(d, seq):
            for i in range(nch):
                yield from chunk(d, seq, i if d == 0 else nch - 1 - i, i == 0, i == nch - 1)

        cur["vmap"] = {0: 0, 1: 1, 2: 2, 3: 3, 4: 3, 5: 0, 6: 1, 7: 2}
        for seq in range(nseq):
            drive([(0, chain(0, seq)), (1, chain(1, seq))])
        cur["vmap"] = None

    def phase_F(l, g):
        barrier()
        N = GN[g]
        nseq, T = (nseq_p, 256) if g == "P" else (1, 4096)
        nch = T // 64
        hv = lambda nm: scr[g, nm].rearrange("(h k) t -> k h t", k=64)
        D = []
        for d in range(2):
            t = dict(FM=A([64, 4, 8, 64]), SG=A([64, 512]), UV=A([128, 8, 64]),
                     ecl=A([64, 8, 64]), encl=A([64, 8, 64]), eclx=A([64, 8, 64]), RT=A([64, 8, 128]),
                     LT=A([64, 8, 128]), M1=A([128, 8, 128]), Q=[A([64, 8, 64]), A([64, 8, 64])],
                     P=[A([64, 8, 64]), A([64, 8, 64])], Rm=[A([64, 8, 64]), A([64, 8, 64])],
                     LTk=A([128, 8, 64]), Xs=A([64, 8, 64]), Ys=A([64, 8, 64]), XL=A([128, 8, 64]), XR=A([128, 8, 64]))
            t["ST"] = t["XR"][0:64, :, :]
            D.append(t)

        def b3(b, parts, inner):
            return bank_(b, parts).rearrange("p (h s) -> p h s", s=inner)

        def chunk(d, seq, ci, first, last):
            t = D[d]
            k = lambda nm: nm + str(d)
            tok = seq * T + ci * 64
            tsl = slice(tok, tok + 64)
            if first:
                if g == "P":
                    S.add("dve", lambda e: e.memset(t["ST"], 0.0), [], [k("ST")])
                else:
                    S.dma(t["ST"], rst_d[l, d].rearrange("p (h v) -> p h v", v=64), ["rst"], [k("ST")])
            for i, nm in enumerate(("rc", "av", "kt%d" % d, "bv%d" % d)):
                src = hv(nm)[:, 0:8, tsl]
                S.dma(t["FM"][:, i, :, :], src, [nm + g], [k("FM")])
            S.dma(t["SG"], scr[g, "sg%d" % d][tsl, :], ["sg%d" % d + g], [k("SG")])
            S.dma(t["UV"][64:128, :, :], scr[g, "vrt"][tsl, :].rearrange("p (h v) -> p h v", v=64), ["vrt" + g], [k("UV")])
            S.dma(t["XR"][64:128, :, :], scr[g, "vrt"][tsl, :].rearrange("p (h v) -> p h v", v=64), ["vrt" + g], [k("VL")])
            if FCUT < 2:
                return
            yield
            for h in range(8):
                mm(bank_(0, 64, 64, h * 64), t["SG"][:, h * 64:(h + 1) * 64], C(("tri2", d), 64, 0, 64), True, True,
                   [k("SG"), "cons"], ["ps0"])
            act(t["ecl"], b3(0, 64, 64), AF.Exp, ["ps0"], [k("ecl")])
            act(t["encl"], b3(0, 64, 64), AF.Exp, ["ps0"], [k("encl")], scale=-1.0)
            if FCUT < 3:
                return
            FMr, FMa, FMk, FMb = (t["FM"][:, i, :, :] for i in range(4))
            if d == 0:
                tt(t["RT"][:, :, 1:64], FMa[:, :, 1:64], t["ecl"][:, :, 0:63], ALU.mult, [k("FM"), k("ecl")], [k("RT")])
                cp(t["RT"][:, :, 0:1], FMa[:, :, 0:1], [k("FM")], [k("RT")])
            else:
                tt(t["RT"][:, :, 0:63], FMa[:, :, 0:63], t["ecl"][:, :, 1:64], ALU.mult, [k("FM"), k("ecl")], [k("RT")])
                cp(t["RT"][:, :, 63:64], FMa[:, :, 63:64], [k("FM")], [k("RT")])
            cp(t["XL"][0:64, :, :], t["RT"][:, :, 0:64], [k("RT")], [k("XL")])
            tt(t["RT"][:, :, 64:128], FMr, t["ecl"], ALU.mult, [k("FM"), k("ecl")], [k("RT")])
            tt(t["LT"][:, :, 0:64], FMb, t["encl"], ALU.mult, [k("FM"), k("encl")], [k("LT")])
            tt(t["LT"][:, :, 64:128], FMk, t["encl"], ALU.mult, [k("FM"), k("encl")], [k("LT")])
            if FCUT < 4:
                return
            yield
            for h in range(8):
                mm(bank_(2 + h // 4, 128, 128, (h % 4) * 128), t["LT"][:, h, :], t["RT"][:, h, :], True, True,
                   [k("LT"), k("RT")], ["ps%d" % (2 + h // 4)])
            for hb in range(2):
                tt(t["M1"][:, 4 * hb:4 * hb + 4, :], b3(2 + hb, 128, 128), C(("m1mask", d)).rearrange("p (h s) -> p h s", s=128),
                   ALU.mult, ["ps%d" % (2 + hb), "cons"], [k("M1")])
            cp(t["XL"][64:128, :, :], t["M1"][64:128, :, 0:64], [k("M1")], [k("XL")])
            if FCUT < 5:
                return
            yield
            for h in range(8):
                mm(bank_(4, 64, 64, h * 64), t["RT"][:, h, 0:64], t["LT"][:, h, 0:64], True, True, [k("LT"), k("RT")], ["ps4"])
            Qc, Pc, Rc = t["Q"][0], t["M1"][0:64, :, 0:64], t["Rm"][0]
            tt(Qc, b3(4, 64, 64), C(("qmask", d), 64).rearrange("p (h s) -> p h s", s=64), ALU.mult, ["ps4", "cons"], [k("Q0")])
            tt(Rc, Pc, bc(id64, 1, [64, 8, 64]), ALU.add, [k("M1"), "cons"], [k("R0")])
            qk_, pk_, rk_ = k("Q0"), k("M1"), k("R0")
            if FCUT < 6:
                return
            yield
            for j in range(1, 6):
                Qn, Pn, Rn = t["Q"][j % 2], t["P"][j % 2], t["Rm"][j % 2]
                qn_, pn_, rn_ = k("Q%d" % (j % 2)), k("P%d" % (j % 2)), k("R%d" % (j % 2))
                yield
                for h in range(8):
                    mm(bank_(6, 64, 64, h * 64), Pc[:, h, :], Qc[:, h, :], True, True, [pk_, qk_], ["ps6"])
                if j < 5:
                    for h in range(8):
                        mm(bank_(5, 64, 64, h * 64), Qc[:, h, :], Pc[:, h, :], True, True, [pk_, qk_], ["ps5"])
                    act(Pn, b3(5, 64, 64), AF.Copy, ["ps5"], [pn_])
                cp(Qn, b3(6, 64, 64), ["ps6"], [qn_])
                yield
                for h in range(8):
                    mm(bank_(7, 64, 64, h * 64), Qn[:, h, :], Rc[:, h, :], True, True, [qn_, rk_], ["ps7"])
                tt(Rn, Rc, b3(7, 64, 64), ALU.add, [rk_, "ps7"], [rn_])
                Qc, Pc, Rc, qk_, pk_, rk_ = Qn, Pn, Rn, qn_, pn_, rn_
            if FCUT < 7:
                return
            yield
            for h in range(8):
                mm(bank_(4, 128, 64, h * 64), t["LT"][:, h, :], id64, True, True, [k("LT"), "cons"], ["ps4"])
            act(t["LTk"], b3(4, 128, 64), AF.Copy, ["ps4"], [k("LTk")])
            if FCUT < 8:
                return
            yield
            for h in range(8):
                mm(bank_(5, 64, 64, h * 64), t["XL"][:, h, :], t["XR"][:, h, :], True, True, [k("XL"), k("ST"), k("VL")], ["ps5"])
            act(t["Xs"], b3(5, 64, 64), AF.Copy, ["ps5"], [k("Xs")])
            yield
            for h in range(8):
                mm(bank_(6, 64, 64, h * 64), Rc[:, h, :], t["Xs"][:, h, :], True, True, [rk_, k("Xs")], ["ps6"])
            cp(t["UV"][0:64, :, :], b3(6, 64, 64), ["ps6"], [k("UV")])
            if FCUT < 9:
                return
            yield
            for h in range(8):
                o_ = bank_(7, 64, 64, h * 64)
                mm(o_, t["ST"][:, h, :], t["RT"][:, h, 64:128], True, False, [k("RT"), k("ST")], ["ps7"])
                mm(o_, t["UV"][:, h, :], t["M1"][:, h, 64:128], False, True, [k("M1"), k("UV")], ["ps7"])
            act(t["Ys"], b3(7, 64, 64), AF.Copy, ["ps7"], [k("Ys")])
            S.dma(hv("y%d" % d)[:, :, tsl], t["Ys"], [k("Ys")], ["y%d" % d + g], eng=STORE_Q)
            if FCUT < 10:
                return
            yield
            for h in range(8):
                mm(bank_(5, 64, 64, h * 64), t["LTk"][:, h, :], t["UV"][:, h, :], True, True, [k("LTk"), k("UV")], ["ps5"])
            tt(t["ST"], t["ST"], b3(5, 64, 64), ALU.add, [k("ST"), "ps5"], [k("ST")])
            li = 63 if d == 0 else 0
            tt(t["ST"], t["ST"], t["ecl"][:, :, li:li + 1].to_broadcast([64, 8, 64]), ALU.mult, [k("ST"), k("ecl")], [k("ST")])
            if last and g == "P":
                S.dma(o_rs[seq, l, d], t["ST"].rearrange("p h v -> p (h v)"), [k("ST")], ["o_rs"], eng=STORE_Q)

        def chain(d, seq):
            for i in range(nch):
                yield from chunk(d, seq, i if d == 0 else nch - 1 - i, i == 0, i == nch - 1)

        cur["vmap"] = {0: 0, 1: 1, 2: 2, 3: 3, 4: 0, 5: 1, 6: 2, 7: 3}
        for seq in range(nseq):
            drive([(0, chain(0, seq)), (1, chain(1, seq))])
        cur["vmap"] = None

    def phase_G(l, g, j, is_last):
        barrier()
        N = GN[g]
        S.dma(ag2, r_g2[l], ["r_g2"], ["ag2"])
        xsrc = (xin[g] if l == 0 else scr[g, "xT"]).rearrange("(c p) t -> p c t", p=128)
        xt = A([128, 8, 512]); mg = A([128, 8, 512]); t2 = A([128, 8, 512]); t3 = A([128, 8, 512]); rt = A([128, 512])
        W = [A([128, 8, 512]), A([128, 8, 512])]
        HID = A([128, 22, 512])
        WF = [A([128, 22, 128]), A([128, 22, 128])]
        HF = A([128, 512]); HB = t3[:, 0, :]; OO = rt; ss = A([128, 4])
        HMT = t2[:, 0:4, :]
        HR = t2[:, 4:8, :]
        fmv = lambda nm: scr[g, nm].rearrange("(c p) t -> p c t", p=128)
        wcnt = [0]; pcnt = [0]

        def nextW():
            i = wcnt[0] % 2; wcnt[0] += 1
            return W[i], "W%d" % i

        def nextP():
            b = pcnt[0] % 6; pcnt[0] += 1
            return bank(b), "ps%d" % b

        for ti in range(N // NT):
            tsl = slice(ti * NT, (ti + 1) * NT)
            S.dma(xt, xsrc[:, :, tsl], ["x_" + g], ["xt"])
            for t4 in range(4):
                tk = slice(ti * NT + t4 * 128, ti * NT + (t4 + 1) * 128)
                S.dma(HF, scr[g, "hm0"][tk, :], ["hm0" + g], ["HF"])
                S.dma(HB, scr[g, "hm1"][tk, :], ["hm1" + g], ["Y0"])
                S.dma(OO, scr[g, "uvo"][tk, 512:1024], ["uvo" + g], ["rt"])
                tt(HF, HF, HB, ALU.add, ["HF", "Y0"], ["HF"])
                tt(HB, HF, HF, ALU.mult, ["HF"], ["Y0"])
                red(ss, HB.rearrange("p (h e) -> p h e", e=128), ALU.add, ["Y0"], ["ss"])
                ts(ss, ss, 1.0 / 128, 1e-6, ALU.mult, ALU.add, ["ss"], ["ss"])
                rsqrt_(ss, "ss")
                hf3 = HF.rearrange("p (h e) -> p h e", e=128)
                tt(hf3, hf3, bc(ss, 2, [128, 4, 128]), ALU.mult, ["HF", "ss"], ["HF"])
                tt(HF, HF, R(("mng", l)), ALU.mult, ["HF", "rows"], ["HF"])
                act(OO, OO, AF.Sigmoid, ["rt"], ["rt"])
                tt(HF, HF, OO, ALU.mult, ["HF", "rt"], ["HF"])
                ps, pk = nextP()
                for c in range(4):
                    mm(ps[:, c * 128:(c + 1) * 128], HF[:, c * 128:(c + 1) * 128], ident, True, True, ["HF", "cons"], [pk])
                act(HMT[:, :, t4 * 128:(t4 + 1) * 128], ps.rearrange("p (c t) -> p c t", t=128), AF.Copy, [pk], ["HMT"])
            S.dma(mg, fmv("mrg")[:, 0:8, tsl], ["mrg" + g], ["mg"])
            w, wk = nextW()
            S.dma(w[:, 0:4, :],
                  proj_m[l].rearrange("(kc p) n -> p kc n", p=128)[:, :, 0:512], ["proj_m"], [wk])
            S.dma(w[:, 4:8, :], proj_m[l].rearrange("(kc p) n -> p kc n", p=128)[:, :, 512:1024], ["proj_m"], [wk])
            for oc in range(8):
                ps, pk = nextP()
                wv = w[:, 0:4, :] if oc < 4 else w[:, 4:8, :]
                for kc in range(4):
                    mm(ps, wv[:, kc, (oc % 4) * 128:(oc % 4 + 1) * 128], HMT[:, kc, :], kc == 0, kc == 3, [wk, "HMT"], [pk])
                tt(mg[:, oc, :], mg[:, oc, :], ps, ALU.mult, ["mg", pk], ["mg"])
            Y0 = t3[:, 0:4, :]; Y1 = t3[:, 4:8, :]
            S.dma(Y0, fmv("y0")[:, :, tsl], ["y0" + g], ["Y0"])
            S.dma(Y1, fmv("y1")[:, :, tsl], ["y1" + g], ["Y1"])
            tt(Y0, Y0, Y1, ALU.add, ["Y0", "Y1"], ["Y0"])
            for c in range(4):
                ps, pk = nextP()
                mm(ps, blk, Y0[:, c, :], True, True, ["cons", "Y0"], [pk])
                stt(Y0[:, c, :], ps, -1.0 / 64, Y0[:, c, :], ALU.mult, ALU.add, [pk, "Y0"], ["Y0"])
            tt(Y1, Y0, Y0, ALU.mult, ["Y0"], ["Y1"])
            for c in range(4):
                ps, pk = nextP()
                mm(ps, blk, Y1[:, c, :], True, True, ["cons", "Y1"], [pk])
                ts(Y1[:, c, :], ps, 1.0 / 64, 64e-5, ALU.mult, ALU.add, [pk], ["Y1"])
            rsqrt_(Y1, "Y1")
            tt(Y0, Y0, Y1, ALU.mult, ["Y0", "Y1"], ["Y0"])
            for c in range(4):
                ts(Y0[:, c, :], Y0[:, c, :], V(("gnw", l), c, 1), V(("gnb", l), c, 1), ALU.mult, ALU.add, ["Y0", "vecs"], ["Y0"])
            S.dma(Y1, fmv("bonus")[:, :, tsl], ["bonus" + g], ["Y1"])
            tt(Y0, Y0, Y1, ALU.add, ["Y0", "Y1"], ["Y0"])
            S.dma(rt, scr[g, "rc"][14 * 128:15 * 128, tsl], ["rc" + g], ["rt"])
            for c in range(4):
                ps, pk = nextP()
                mm(ps, ag2[:, c * 128:(c + 1) * 128], rt, True, True, ["ag2", "rt"], [pk])
                tt(HR[:, c, :], Y0[:, c, :], ps, ALU.mult, ["Y0", pk], ["HR"])
            S.dma(t3, fmv("mrg")[:, 8:16, tsl], ["mrg" + g], ["Y0", "Y1"])
            w, wk = nextW()
            S.dma(w[:, 0:4, :], proj_r[l].rearrange("(kc p) n -> p kc n", p=128)[:, :, 0:512], ["proj_r"], [wk])
            S.dma(w[:, 4:8, :], proj_r[l].rearrange("(kc p) n -> p kc n", p=128)[:, :, 512:1024], ["proj_r"], [wk])
            for oc in range(8):
                ps, pk = nextP()
                wv = w[:, 0:4, :] if oc < 4 else w[:, 4:8, :]
                for kc in range(4):
                    mm(ps, wv[:, kc, (oc % 4) * 128:(oc % 4 + 1) * 128], HR[:, kc, :], kc == 0, kc == 3, [wk, "HR"], [pk])
                tt(t3[:, oc, :], t3[:, oc, :], ps, ALU.mult, ["Y0", "Y1", pk], ["Y0", "Y1"])
            tt(mg, mg, t3, ALU.add, ["mg", "Y0", "Y1"], ["mg"])
            for half in range(2):
                w, wk = nextW()
                S.dma(w, w_out[l].rearrange("(kc p) n -> p kc n", p=128)[:, :, half * 512:(half + 1) * 512], ["w_out"], [wk])
                for o4 in range(4):
                    oc = half * 4 + o4
                    ps, pk = nextP()
                    for kc in range(8):
                        mm(ps, w[:, kc, o4 * 128:(o4 + 1) * 128], mg[:, kc, :], kc == 0, kc == 7, [wk, "mg"], [pk])
                    stt(xt[:, oc, :], ps, modt[l][:, 16 + oc, j:j + 1], xt[:, oc, :], ALU.mult, ALU.add, [pk, "xt", "mod"], ["xt"])
            rmsnorm_tile(xt, "xt", mg, "mg", t2, "HMT", rt, "rt",
                         lambda c: A2t[l][:, c, j:j + 1], lambda c: modt[l][:, 24 + c, j:j + 1])
            w1v = ffn_w1[l].rearrange("(kc p) n -> p kc n", p=128)
            w3v = ffn_w3[l].rearrange("(kc p) n -> p kc n", p=128)
            for sg_ in range(0, 22, 4):
                ncc = min(4, 22 - sg_)
                wa, wak = nextW()
                S.dma(wa[:, :, 0:ncc * 128], w1v[:, :, sg_ * 128:(sg_ + ncc) * 128], ["ffn_w1"], [wak])
                wb, wbk = nextW()
                S.dma(wb[:, :, 0:ncc * 128], w3v[:, :, sg_ * 128:(sg_ + ncc) * 128], ["ffn_w3"], [wbk])
                for cc in range(ncc):
                    hc = sg_ + cc
                    p1, p1k = nextP()
                    for kc in range(8):
                        mm(p1, wa[:, kc, cc * 128:(cc + 1) * 128], mg[:, kc, :], kc == 0, kc == 7, [wak, "mg"], [p1k])
                    p3, p3k = nextP()
                    for kc in range(8):
                        mm(p3, wb[:, kc, cc * 128:(cc + 1) * 128], mg[:, kc, :], kc == 0, kc == 7, [wbk, "mg"], [p3k])
                    act(HID[:, hc, :], p1, AF.Silu, [p1k], ["HID"])
                    tt(HID[:, hc, :], HID[:, hc, :], p3, ALU.mult, ["HID", p3k], ["HID"])
            w2v = ffn_w2[l].rearrange("(kc p) n -> p kc n", p=128)
            for oc in range(8):
                wf = WF[oc % 2]; wfk = "WF%d" % (oc % 2)
                S.dma(wf, w2v[:, :, oc * 128:(oc + 1) * 128], ["ffn_w2"], [wfk])
                ps, pk = nextP()
                for kc in range(22):
                    mm(ps, wf[:, kc, :], HID[:, kc, :], kc == 0, kc == 21, [wfk, "HID"], [pk])
                stt(xt[:, oc, :], ps, modt[l][:, 40 + oc, j:j + 1], xt[:, oc, :], ALU.mult, ALU.add, [pk, "xt", "mod"], ["xt"])
            if is_last:
                rmsnorm_tile(xt, "xt", mg, "mg", t2, "HMT", rt, "rt", lambda c: V("fg", c, 1), None)
                S.dma(yout[g].rearrange("(c p) t -> p c t", p=128)[:, :, tsl], mg, ["mg"], ["yout" + g], eng=STORE_Q)
            else:
                S.dma(scr[g, "xT"].rearrange("(c p) t -> p c t", p=128)[:, :, tsl], xt, ["xt"], ["x_" + g], eng=STORE_Q)

    for l in range(nlayers):
        if "A" in phases:
            phase_A(l)
        for (g, j) in (("P", 0), ("S", 1)):
            if g not in groups:
                continue
            if "B" in phases:
                phase_B(l, g, j)
            if "C" in phases:
                phase_C(l, g)
            if "D" in phases:
                phase_D(l, g)
            if "E" in phases:
                phase_E(l, g)
            if "F" in phases:
                phase_F(l, g)
            if "G" in phases:
                phase_G(l, g, j, l == nlayers - 1)
    S.finish()
    return nc, len(S.ops)


def make_in_maps(inp):
    f = lambda a: np.ascontiguousarray(np.asarray(a, np.float32))
    shared = {k: f(inp[k]) for k in ("ada_w", "w_in", "proj_m", "proj_r", "w_out", "ffn_w1", "ffn_w3", "ffn_w2")}
    shared["r_w2"] = f(inp["r_w2"]).reshape(DEPTH, 128, 512)
    shared["r_a2"] = f(inp["r_a2"]).reshape(DEPTH, 128, 512)
    shared["r_g2"] = f(inp["r_g2"])
    shared["vecs"] = pack_vecs(inp)
    shared["rows"] = pack_rows(inp)
    shared["consts"] = make_consts()
    maps = []
    xp = f(inp["x_prompt"])
    xs = f(inp["x_sample"])
    for c in range(8):
        b = c // 4
        m = dict(shared)
        m["xpT"] = np.ascontiguousarray(xp[4 * c:4 * c + 4].reshape(NP_, DM).T)
        m["xsT"] = np.ascontiguousarray(xs[b].T)
        cv = np.stack([fm(inp["c_ctx"], 8), fm(inp["c"][b], 8)], axis=2)
        m["cvec"] = np.ascontiguousarray(cv.reshape(128, 16))
        C = f(inp["state_mlstm_c"])[b].transpose(0, 1, 3, 2, 4)
        n = f(inp["state_mlstm_n"])[b].transpose(0, 1, 3, 2)[..., None]
        m["cst"] = np.ascontiguousarray(np.concatenate([C, n], axis=-1).reshape(DEPTH, 2, 128, 4 * 129))
        m["mst"] = np.ascontiguousarray(np.broadcast_to(f(inp["state_mlstm_m"])[b][:, :, None, :], (DEPTH, 2, 128, 4)))
        m["rst"] = np.ascontiguousarray(f(inp["state_rwkv"])[b].transpose(0, 1, 4, 2, 3).reshape(DEPTH, 2, 64, 512))
        maps.append(m)
    return maps


_CACHE = {}


def kernel(**inputs):
    if "nc" not in _CACHE:
        _CACHE["nc"] = build()[0]
    nc = _CACHE["nc"]
    maps = make_in_maps(inputs)
    res = run_bass_kernel_spmd(nc, maps, core_ids=list(range(8))).results
    yp = np.concatenate([res[c]["ypT"].T.reshape(4, 256, DM) for c in range(8)], axis=0)
    ys = np.stack([res[0]["ysT"].T, res[4]["ysT"].T], axis=0)
    mc = np.concatenate([res[c]["o_mc"].reshape(4, DEPTH, 2, 128, 4, 129) for c in range(8)], axis=0)
    new_c = np.ascontiguousarray(mc[..., :128].transpose(0, 1, 2, 4, 3, 5))
    new_n = np.ascontiguousarray(mc[..., 128].transpose(0, 1, 2, 4, 3))
    new_m = np.concatenate([res[c]["o_mm"] for c in range(8)], axis=0)
    rs = np.concatenate([res[c]["o_rs"].reshape(4, DEPTH, 2, 64, 8, 64) for c in range(8)], axis=0)
    new_s = np.ascontiguousarray(rs.transpose(0, 1, 2, 4, 5, 3))
    return (yp.astype(np.float32), ys.astype(np.float32), new_c.astype(np.float32), new_n.astype(np.float32),
            new_m.astype(np.float32), new_s.astype(np.float32))
```
